# Optimizing a Trainium2 kernel written in Bass

```python
import jax, jax.numpy as jnp
from jax import lax
import numpy as np

D_MODEL = 1024
BATCH = 2
SEQ = 8192
DEPTH = 2

D_FF = 2816
CONV_WIDTH = 512
CONV_K = 31
SGU_WIDTH = 512
SGU_GROUPS = 4
SGU_CHUNK = 128
ATT_HEADS = 8
HEAD_DIM = 64
ATT_WIDTH = ATT_HEADS * HEAD_DIM
MOBA_BLOCK = 256
MOBA_TOPK = 3
Q_CHUNK = 64
ROPE_THETA = 500000.0
ROPE_DIM = HEAD_DIM // 4
N_BRANCH = 3
EPS = 1e-6
N_IN = 2 * CONV_WIDTH + 2 * SGU_WIDTH + 3 * ATT_WIDTH + N_BRANCH * D_MODEL

kernel_name = "hybrid_gated_conv_gmlp_moba_macaron"


def _rmsnorm(x, g):
    xf = x.astype(jnp.float32)
    y = xf * lax.rsqrt(jnp.mean(xf * xf, axis=-1, keepdims=True) + EPS)
    return (y * g.astype(jnp.float32)).astype(x.dtype)


def _layernorm(x, g, b):
    xf = x.astype(jnp.float32)
    mu = jnp.mean(xf, axis=-1, keepdims=True)
    var = jnp.mean(jnp.square(xf - mu), axis=-1, keepdims=True)
    y = (xf - mu) * lax.rsqrt(var + EPS)
    return (y * g.astype(jnp.float32) + b.astype(jnp.float32)).astype(x.dtype)


def _swiglu(x, wi, wo):
    gate, up = jnp.split(x @ wi, 2, axis=-1)
    return (jax.nn.silu(gate) * up) @ wo


def _rope_tables(seq):
    pos = jnp.arange(seq, dtype=jnp.float32)
    inv_freq = ROPE_THETA ** (-jnp.arange(0, ROPE_DIM, 2, dtype=jnp.float32) / ROPE_DIM)
    ang = pos[:, None] * inv_freq[None, :]
    return jnp.cos(ang), jnp.sin(ang)


def _partial_rope(x, cos, sin):
    c = cos[None, :, None, :].astype(x.dtype)
    s = sin[None, :, None, :].astype(x.dtype)
    half = ROPE_DIM // 2
    x1 = x[..., :half]
    x2 = x[..., half:ROPE_DIM]
    return jnp.concatenate([x1 * c - x2 * s, x2 * c + x1 * s, x[..., ROPE_DIM:]], axis=-1)


def _conv_module(z, conv_w, conv_b, ln_g, ln_b):
    a, g = jnp.split(z, 2, axis=-1)
    y = a * jax.nn.sigmoid(g)
    y = lax.conv_general_dilated(
        y, conv_w[:, None, :], window_strides=(1,),
        padding=[(CONV_K - 1, 0)],
        dimension_numbers=("NWC", "WIO", "NWC"),
        feature_group_count=CONV_WIDTH) + conv_b
    y = _layernorm(y, ln_g, ln_b)
    return jax.nn.silu(y)


def _sgu_module(z, ln_g, ln_b, w_s, b_s):
    u, v = jnp.split(jax.nn.gelu(z), 2, axis=-1)
    v = _layernorm(v, ln_g, ln_b)
    bsz, seq, _ = v.shape
    vc = v.reshape(bsz, seq // SGU_CHUNK, SGU_CHUNK, SGU_GROUPS, SGU_WIDTH // SGU_GROUPS)
    causal = jnp.tril(jnp.ones((SGU_CHUNK, SGU_CHUNK), dtype=bool))
    w = jnp.where(causal[None], w_s, 0.0)
    mixed = jnp.einsum("gts,bnsgc->bntgc", w, vc) + b_s.T[None, None, :, :, None]
    return u * mixed.reshape(bsz, seq, SGU_WIDTH)


def _moba_attention(q, k, v):
    bsz, seq, nh, dh = q.shape
    s_pad = -(-seq // MOBA_BLOCK) * MOBA_BLOCK
    pad = [(0, 0), (0, s_pad - seq), (0, 0), (0, 0)]
    qh = jnp.pad(q, pad).transpose(0, 2, 1, 3)
    kh = jnp.pad(k, pad).transpose(0, 2, 1, 3)
    vh = jnp.pad(v, pad).transpose(0, 2, 1, 3)
    nb = s_pad // MOBA_BLOCK
    kb = kh.reshape(bsz, nh, nb, MOBA_BLOCK, dh)
    vb = vh.reshape(bsz, nh, nb, MOBA_BLOCK, dh)

    kmean = jnp.mean(kb.astype(jnp.float32), axis=3).astype(q.dtype)
    gate = jnp.einsum("bhsd,bhnd->bhsn", qh, kmean).astype(jnp.float32)
    qblk = jnp.arange(s_pad) // MOBA_BLOCK
    past = jnp.arange(nb)[None, :] < qblk[:, None]
    gate = jnp.where(past[None, None], gate, -jnp.inf)
    k_top = min(MOBA_TOPK, nb)
    _, sel = lax.top_k(gate, k_top)
    sel_valid = sel < qblk[None, None, :, None]
    own = jnp.broadcast_to(qblk[None, None, :, None], (bsz, nh, s_pad, 1))
    idx = jnp.concatenate([sel, own.astype(sel.dtype)], axis=-1)
    n_sel = k_top + 1

    n_chunks = s_pad // Q_CHUNK
    q_c = qh.reshape(bsz, nh, n_chunks, Q_CHUNK, dh).transpose(2, 0, 1, 3, 4)
    idx_c = idx.reshape(bsz, nh, n_chunks, Q_CHUNK, n_sel).transpose(2, 0, 1, 3, 4)
    val_c = sel_valid.reshape(bsz, nh, n_chunks, Q_CHUNK, k_top).transpose(2, 0, 1, 3, 4)
    bi = jnp.arange(bsz)[:, None, None, None]
    hi = jnp.arange(nh)[None, :, None, None]
    key_off = jnp.arange(MOBA_BLOCK)
    scale = HEAD_DIM ** -0.5

    def step(args):
        qc, ic, vc, c = args
        kg = kb[bi, hi, ic]
        vg = vb[bi, hi, ic]
        s = jnp.einsum("bhqd,bhqnkd->bhqnk", qc, kg).astype(jnp.float32) * scale
        qpos = c * Q_CHUNK + jnp.arange(Q_CHUNK)
        own_kpos = ic[..., -1:, None] * MOBA_BLOCK + key_off
        own_mask = own_kpos <= qpos[None, None, :, None, None]
        sel_mask = jnp.broadcast_to(vc[..., None], (bsz, nh, Q_CHUNK, k_top, MOBA_BLOCK))
        mask = jnp.concatenate([sel_mask, own_mask], axis=3)
        s = jnp.where(mask, s, -jnp.inf)
        p = jax.nn.softmax(s.reshape(bsz, nh, Q_CHUNK, n_sel * MOBA_BLOCK), axis=-1)
        p = p.reshape(bsz, nh, Q_CHUNK, n_sel, MOBA_BLOCK).astype(vg.dtype)
        return jnp.einsum("bhqnk,bhqnkd->bhqd", p, vg)

    out = lax.map(step, (q_c, idx_c, val_c, jnp.arange(n_chunks)))
    out = out.transpose(1, 0, 3, 2, 4).reshape(bsz, s_pad, nh * dh)
    return out[:, :seq]


def _layer(x, cos, sin, ffn1_norm, ffn1_wi, ffn1_wo, mix_norm, w_in, conv_w, conv_b,
           conv_ln_g, conv_ln_b, sgu_ln_g, sgu_ln_b, sgu_w, sgu_b, w_branch, gate_b,
           w_out, ffn2_norm, ffn2_wi, ffn2_wo):
    bsz, seq, _ = x.shape
    x = x + 0.5 * _swiglu(_rmsnorm(x, ffn1_norm), ffn1_wi, ffn1_wo)

    h = _rmsnorm(x, mix_norm)
    z = h @ w_in
    splits = list(np.cumsum([2 * CONV_WIDTH, 2 * SGU_WIDTH, ATT_WIDTH, ATT_WIDTH, ATT_WIDTH]))
    z_conv, z_sgu, z_q, z_k, z_v, z_gate = jnp.split(z, splits, axis=-1)

    y_a = _conv_module(z_conv, conv_w, conv_b, conv_ln_g, conv_ln_b)
    y_b = _sgu_module(z_sgu, sgu_ln_g, sgu_ln_b, sgu_w, sgu_b)
    q = _partial_rope(z_q.reshape(bsz, seq, ATT_HEADS, HEAD_DIM), cos, sin)
    k = _partial_rope(z_k.reshape(bsz, seq, ATT_HEADS, HEAD_DIM), cos, sin)
    v = z_v.reshape(bsz, seq, ATT_HEADS, HEAD_DIM)
    y_c = _moba_attention(q, k, v)

    branches = jnp.stack([y_a, y_b, y_c], axis=2)
    proj = jnp.einsum("bsnc,ncd->bsnd", branches, w_branch)
    gates = jax.nn.sigmoid(z_gate.reshape(bsz, seq, N_BRANCH, D_MODEL) + gate_b)
    merged = jnp.sum(gates * proj, axis=2)
    x = x + merged @ w_out

    x = x + 0.5 * _swiglu(_rmsnorm(x, ffn2_norm), ffn2_wi, ffn2_wo)
    return x


def setup_inputs(seed: int = 0) -> dict:
    key = jax.random.key(seed)
    ks = jax.random.split(key, 24)
    L, D = DEPTH, D_MODEL

    def nrm(k, shape, scale):
        return jax.random.normal(k, shape, jnp.float32) * scale

    return {
        "x": nrm(ks[0], (BATCH, SEQ, D), 1.0),
        "ffn1_norm": 1.0 + nrm(ks[1], (L, D), 0.01),
        "ffn1_wi": nrm(ks[2], (L, D, 2 * D_FF), D ** -0.5),
        "ffn1_wo": nrm(ks[3], (L, D_FF, D), D_FF ** -0.5),
        "mix_norm": 1.0 + nrm(ks[4], (L, D), 0.01),
        "w_in": nrm(ks[5], (L, D, N_IN), D ** -0.5),
        "conv_w": nrm(ks[6], (L, CONV_K, CONV_WIDTH), CONV_K ** -0.5),
        "conv_b": nrm(ks[7], (L, CONV_WIDTH), 0.01),
        "conv_ln_g": 1.0 + nrm(ks[8], (L, CONV_WIDTH), 0.01),
        "conv_ln_b": nrm(ks[9], (L, CONV_WIDTH), 0.01),
        "sgu_ln_g": 1.0 + nrm(ks[10], (L, SGU_WIDTH), 0.01),
        "sgu_ln_b": nrm(ks[11], (L, SGU_WIDTH), 0.01),
        "sgu_w": nrm(ks[12], (L, SGU_GROUPS, SGU_CHUNK, SGU_CHUNK), SGU_CHUNK ** -0.5),
        "sgu_b": 1.0 + nrm(ks[13], (L, SGU_GROUPS, SGU_CHUNK), 0.01),
        "w_branch": nrm(ks[14], (L, N_BRANCH, CONV_WIDTH, D), CONV_WIDTH ** -0.5),
        "gate_b": nrm(ks[15], (L, N_BRANCH, D), 0.01),
        "w_out": nrm(ks[16], (L, D, D), D ** -0.5),
        "ffn2_norm": 1.0 + nrm(ks[17], (L, D), 0.01),
        "ffn2_wi": nrm(ks[18], (L, D, 2 * D_FF), D ** -0.5),
        "ffn2_wo": nrm(ks[19], (L, D_FF, D), D_FF ** -0.5),
        "final_norm": 1.0 + nrm(ks[20], (D,), 0.01),
    }


def reference(x, ffn1_norm, ffn1_wi, ffn1_wo, mix_norm, w_in, conv_w, conv_b, conv_ln_g,
              conv_ln_b, sgu_ln_g, sgu_ln_b, sgu_w, sgu_b, w_branch, gate_b, w_out,
              ffn2_norm, ffn2_wi, ffn2_wo, final_norm):
    cos, sin = _rope_tables(x.shape[1])
    for l in range(DEPTH):
        x = _layer(x, cos, sin, ffn1_norm[l], ffn1_wi[l], ffn1_wo[l], mix_norm[l], w_in[l],
                   conv_w[l], conv_b[l], conv_ln_g[l], conv_ln_b[l], sgu_ln_g[l],
                   sgu_ln_b[l], sgu_w[l], sgu_b[l], w_branch[l], gate_b[l], w_out[l],
                   ffn2_norm[l], ffn2_wi[l], ffn2_wo[l])
    return _rmsnorm(x, final_norm)
```

```python
import contextlib
import numpy as np
import ml_dtypes
import concourse.bass as bass
import concourse.mybir as mybir
from concourse.bass_utils import run_bass_kernel_spmd

F32, BF16 = mybir.dt.float32, mybir.dt.bfloat16
AF = mybir.ActivationFunctionType
ALU = mybir.AluOpType
AX = mybir.AxisListType

D = 1024
DFF = 2816
SEQ = 8192
NTOK = 2048
TG = 512
NG = NTOK // TG
EPS = 1e-6
NPP = 160
NEG = -30000.0
GELU_TANH_NATIVE = False


class T:
    __slots__ = ("name", "w", "r", "psum")

    def __init__(self, name, psum=False):
        self.name, self.w, self.r, self.psum = name, None, {}, psum


class Sched:
    ENG = ("pe", "act", "dve", "pool", "sp")

    def __init__(self, nc):
        self.nc = nc
        self.ops = []
        self.keytot = {}

    def op(self, eng, fn, reads=(), writes=(), key=None):
        idx = len(self.ops)
        is_dma = key is not None
        deps = set()

        def add(d, kind):
            o = self.ops[d]
            if o["key"] is not None:
                deps.add(("k", o["key"], self.keytot[o["key"]]))
            else:
                if o["eng"] == eng and not is_dma:
                    if eng == "pe" or kind == "war":
                        return
                deps.add(("e", d))

        for t in reads:
            if t.w is not None:
                add(t.w, "raw")
            if t.psum:
                for rk_, d in t.r.items():
                    if rk_ != eng:
                        add(d, "raw")
        for t in writes:
            if t.w is not None:
                add(t.w, "waw")
            for d in t.r.values():
                add(d, "war")
        if is_dma:
            self.keytot[key] = self.keytot.get(key, 0) + 16
        self.ops.append(dict(eng=eng, fn=fn, deps=deps, key=key))
        rk = ("k", key) if is_dma else eng
        for t in reads:
            t.r[rk] = idx
        for t in writes:
            t.w = idx
            t.r = {}
        return idx

    def emit(self, stack, final_keys, tag=""):
        nc = self.nc
        need = set(d[1] for o in self.ops for d in o["deps"] if d[0] == "e")
        cnt = {e: 0 for e in self.ENG}
        ms = {}
        for i, o in enumerate(self.ops):
            if i in need:
                cnt[o["eng"]] += 1
                ms[i] = cnt[o["eng"]]
        assert max(cnt.values()) < 60000, cnt
        sem_e = {e: stack.enter_context(nc.semaphore(f"se_{tag}{e}")) for e in self.ENG}
        sem_k = {k: stack.enter_context(nc.semaphore(f"sk_{tag}{k}")) for k in self.keytot}
        per = {e: [] for e in self.ENG}
        for i, o in enumerate(self.ops):
            per[o["eng"]].append(i)
        ops = self.ops
        keytot = self.keytot

        def run(e, eo):
            seen_e = {f: 0 for f in self.ENG}
            seen_k = {}
            for i in per[e]:
                o = ops[i]
                for d in sorted(o["deps"], key=str):
                    if d[0] == "e":
                        f = ops[d[1]]["eng"]
                        v = ms[d[1]]
                        if seen_e[f] < v:
                            eo.wait_ge(sem_e[f], v)
                            seen_e[f] = v
                    else:
                        _, k, v = d
                        if seen_k.get(k, 0) < v:
                            eo.wait_ge(sem_k[k], v)
                            seen_k[k] = v
                ins = o["fn"](eo)
                if o["key"] is not None:
                    ins.then_inc(sem_k[o["key"]], 16)
                elif i in ms:
                    ins.then_inc(sem_e[e], 1)
            if e == "sp":
                for k in final_keys:
                    eo.wait_ge(sem_k[k], keytot[k])

        with nc.Block() as block:
            @block.tensor
            def _(eo):
                run("pe", eo)

            @block.scalar
            def _(eo):
                run("act", eo)

            @block.vector
            def _(eo):
                run("dve", eo)

            @block.gpsimd
            def _(eo):
                run("pool", eo)

            @block.sync
            def _(eo):
                run("sp", eo)


class Ctx:
    def __init__(self, nc, stack):
        self.nc, self.stack = nc, stack
        self.s = None
        self.psum = []
        for i in range(8):
            t = stack.enter_context(nc.psum_tensor(f"ps{i}", [128, 512], F32))
            self.psum.append((T(f"ps{i}", psum=True), t))
        self.pi = 0
        self.nstage = 0

    def begin(self):
        self.s = Sched(self.nc)
        for t, _ in self.psum:
            t.w, t.r = None, {}

    def end(self, final_keys):
        self.s.emit(self.stack, final_keys, tag=f"s{self.nstage}_")
        self.nstage += 1

    def ps(self):
        r = self.psum[self.pi % getattr(self, "ps_n", 8)]
        self.pi += 1
        return r

    def sb(self, stack, name, shape, dt):
        t = stack.enter_context(self.nc.sbuf_tensor(name, shape, dt))
        return T(name), t

    def rot(self, stack, name, shape, dt, n):
        return Rot([self.sb(stack, f"{name}{i}", shape, dt) for i in range(n)])

    def op(self, *a, **k):
        return self.s.op(*a, **k)

    def dma(self, out, in_, reads, writes, key, q="sp"):
        self.s.op(q, lambda e: e.dma_start(out=out, in_=in_), reads, writes, key=key)

    def mm(self, pt, out, lhsT, rhs, start, stop, reads):
        self.s.op("pe", lambda e: e.matmul(out=out, lhsT=lhsT, rhs=rhs, start=start, stop=stop),
                  reads, [pt])

    def tr(self, pt, out, in_, ident, reads):
        self.s.op("pe", lambda e: e.transpose(out=out, in_=in_, identity=ident), reads, [pt])

    def act(self, out, in_, func, reads, writes, bias=None, scale=None):
        kw = {}
        if bias is not None:
            kw["bias"] = bias
        if scale is not None:
            kw["scale"] = scale
        self.s.op("act", lambda e: e.activation(out=out, in_=in_, func=func, **kw), reads, writes)

    def tt(self, out, in0, in1, op, reads, writes, eng="dve"):
        self.s.op(eng, lambda e: e.tensor_tensor(out=out, in0=in0, in1=in1, op=op), reads, writes)

    def ts(self, out, in0, s1, s2, op0, op1, reads, writes, eng="dve"):
        if op1 is None:
            self.s.op(eng, lambda e: e.tensor_scalar(out=out, in0=in0, scalar1=s1, scalar2=None, op0=op0),
                      reads, writes)
        else:
            self.s.op(eng, lambda e: e.tensor_scalar(out=out, in0=in0, scalar1=s1, scalar2=s2, op0=op0, op1=op1),
                      reads, writes)

    def stt(self, out, in0, scalar, in1, op0, op1, reads, writes):
        self.s.op("dve", lambda e: e.scalar_tensor_tensor(out=out, in0=in0, scalar=scalar, in1=in1,
                                                          op0=op0, op1=op1), reads, writes)

    def cp(self, out, in_, reads, writes, eng="dve"):
        if eng == "act":
            self.s.op("act", lambda e: e.copy(out=out, in_=in_), reads, writes)
        else:
            self.s.op(eng, lambda e: e.tensor_copy(out=out, in_=in_), reads, writes)

    def memset(self, out, val, writes, eng="dve"):
        self.s.op(eng, lambda e: e.memset(out, val), [], writes)


class Rot:
    def __init__(self, items):
        self.items, self.i = items, 0

    def next(self):
        r = self.items[self.i % len(self.items)]
        self.i += 1
        return r


def din(nc, name, shape, dt):
    return nc.dram_tensor(name, list(shape), dt, kind="ExternalInput").ap()


def dout(nc, name, shape, dt):
    return nc.dram_tensor(name, list(shape), dt, kind="ExternalOutput").ap()


def build_att():
    nc = bass.Bass("TRN2", target_bir_lowering=False)
    qT = din(nc, "qT", [2, 64, SEQ], BF16)
    kT = din(nc, "kT", [2, 64, SEQ], BF16)
    vv = din(nc, "v", [2, SEQ, 64], BF16)
    kaug = din(nc, "kaug", [32, SEQ], BF16)
    pastneg = din(nc, "pastneg", [1, 2048], F32)
    ownm = din(nc, "ownm", [1, 2048], F32)
    cmask = din(nc, "cmask", [4, 128, 512], BF16)
    identf_d = din(nc, "identf", [128, 128], F32)
    yc = dout(nc, "yc", [2, SEQ, 64], BF16)
    with contextlib.ExitStack() as st:
        c = Ctx(nc, st)
        c.begin()
        QA = [c.sb(st, f"QA{i}", [96, SEQ], BF16) for i in range(2)]
        QB = [T(f"QAb{i}") for i in range(2)]
        KA = [c.sb(st, f"KA{i}", [96, SEQ], BF16) for i in range(2)]
        VA = [c.sb(st, f"VA{i}", [128, 64, 65], BF16) for i in range(2)]
        pn_t, pn = c.sb(st, "pastneg_sb", [128, 64, 32], F32)
        ow_t, ow = c.sb(st, "own_sb", [128, 64, 32], F32)
        cm_t, cm = c.sb(st, "cm_sb", [128, 4, 512], BF16)
        idf_t, idf = c.sb(st, "identf_sb", [128, 128], F32)
        idb_t, idb = c.sb(st, "identb_sb", [128, 128], BF16)
        kmf_t, kmf = c.sb(st, "kmf", [64, 32], F32)
        qf_t, qf = c.sb(st, "qf", [64, SEQ], F32)
        gm_r = c.rot(st, "gm", [128, 4, 32], F32, 3)
        m8_r = c.rot(st, "m8", [128, 4, 8], F32, 3)
        thr_r = c.rot(st, "thr", [128, 4], F32, 3)
        b2_r = c.rot(st, "b2", [128, 4, 32], F32, 3)
        b96_r = c.rot(st, "b96", [128, 4, 96], F32, 3)
        pT_r = c.rot(st, "pT", [128, 512], BF16, 4)
        rc_r = c.rot(st, "rc", [128, 1], F32, 4)
        yo_r = c.rot(st, "yo", [128, 4, 64], BF16, 2)

        c.dma(pn[:].rearrange("p a b -> p (a b)"), pastneg.to_broadcast([128, 2048]), [], [pn_t], "pn")
        c.dma(ow[:].rearrange("p a b -> p (a b)"), ownm.to_broadcast([128, 2048]), [], [ow_t], "ow")
        c.dma(cm[:], cmask.rearrange("i p q -> p i q"), [], [cm_t], "cm")
        c.dma(idf[:], identf_d, [], [idf_t], "idf")
        c.cp(idb[:], idf[:], [idf_t], [idb_t])
        for (bt, b96) in b96_r.items:
            c.memset(b96[:], 0.0, [bt])
        for i in range(2):
            c.dma(KA[i][1][64:96, :], kaug, [], [KA[i][0]], f"KA{i}")
            c.memset(VA[i][1][:, :, 64:65], 1.0, [VA[i][0]])

        def load(b):
            qa_t, qa = QA[b % 2]
            ka_t, ka = KA[b % 2]
            va_t, va = VA[b % 2]
            c.dma(qa[0:64, :], qT[b], [], [qa_t], f"QA{b % 2}")
            c.dma(ka[0:64, :], kT[b], [], [ka_t], f"KA{b % 2}")
            vsrc = vv[b].rearrange("(n p) d -> p n d", p=128)
            for j in range(8):
                c.dma(va[:, j * 8:(j + 1) * 8, 0:64], vsrc[:, j * 8:(j + 1) * 8, :], [], [va_t], f"VA{b % 2}")

        load(0)
        for b in range(2):
            qa_t, qa = QA[b % 2]
            qb_t = QB[b % 2]
            ka_t, ka = KA[b % 2]
            va_t, va = VA[b % 2]
            if b + 1 < 2:
                load(b + 1)
            for hq in range(4):
                c.cp(qf[:, hq * 2048:(hq + 1) * 2048], qa[0:64, hq * 2048:(hq + 1) * 2048], [qa_t], [qf_t],
                     eng=("pool" if hq % 2 == 0 else "dve"))
            c.op("dve", lambda e, ka=ka: e.tensor_reduce(
                out=kmf[:], in_=ka[0:64, :].rearrange("p (j k) -> p j k", k=256), axis=AX.X, op=ALU.add),
                [ka_t], [kmf_t])
            c.ts(kmf[:], kmf[:], 1.0 / 256.0, None, ALU.mult, None, [kmf_t], [kmf_t])
            stA, stB = {}, {}

            def gA(G_):
                pg_t, pg = c.ps()
                for j in range(4):
                    qt = 4 * G_ + j
                    c.mm(pg_t, pg[:, j * 32:(j + 1) * 32], qf[:, qt * 128:(qt + 1) * 128], kmf[:], True, True,
                         [qf_t, kmf_t])
                stA[G_] = (pg_t, pg)

            def gB(G_):
                pg_t, pg = stA.pop(G_)
                gm_t, gm = gm_r.next()
                c.tt(gm[:], pg[:, 0:128].rearrange("p (j n) -> p j n", n=32), pn[:, 4 * G_:4 * G_ + 4, :], ALU.add,
                     [pg_t, pn_t], [gm_t])
                m8_t, m8 = m8_r.next()
                for j in range(4):
                    c.op("dve", lambda e, m8=m8, gm=gm, j=j: e.max(out=m8[:, j, :], in_=gm[:, j, :]), [gm_t], [m8_t])
                th_t, th = thr_r.next()
                c.ts(th[:], m8[:, :, 2], -1e29, None, ALU.max, None, [m8_t], [th_t])
                b2_t, b2 = b2_r.next()
                for j in range(4):
                    c.stt(b2[:, j, :], gm[:, j, :], th[:, j:j + 1], ow[:, 4 * G_ + j, :], ALU.is_ge, ALU.add,
                          [gm_t, th_t, ow_t], [b2_t])
                b96_t, b96 = b96_r.next()
                c.ts(b96[:, :, 64:96], b2[:], -NEG, NEG, ALU.mult, ALU.add, [b2_t], [b96_t])
                stB[G_] = (b96_t, b96)

            def gC(G_):
                b96_t, b96 = stB.pop(G_)
                pt_t, pt = c.ps()
                for j in range(4):
                    c.tr(pt_t, pt[0:96, j * 128:(j + 1) * 128], b96[:, j, :], idf[:], [b96_t, idf_t])
                c.cp(qa[64:96, G_ * 512:(G_ + 1) * 512], pt[64:96, :], [pt_t], [qb_t], eng="act")

            for step in range(16 + 2):
                if step < 16:
                    gA(step)
                if 0 <= step - 1 < 16:
                    gB(step - 1)
                if 0 <= step - 2 < 16:
                    gC(step - 2)
            units = [(G, kt) for G in range(16) for kt in range(4 * G + 4)]
            po = [c.psum[4 + i] for i in range(4)]
            sq = {}

            def qk(u, ka=ka, qa=qa, ka_t=ka_t, qa_t=qa_t, qb_t=qb_t):
                G, kt = units[u]
                s_t, s_ = c.psum[u % 4]
                own = kt >= 4 * G
                c.mm(s_t, s_[:, :], ka[0:96, kt * 128:(kt + 1) * 128], qa[0:96, G * 512:(G + 1) * 512],
                     True, not own, [ka_t, qa_t, qb_t])
                if own:
                    c.mm(s_t, s_[:, :], idb[:], cm[:, kt - 4 * G, :], False, True, [idb_t, cm_t])
                sq[u] = (s_t, s_)

            LA = 3
            for u0 in range(LA):
                qk(u0)
            for u in range(len(units)):
                G, kt = units[u]
                nkt = 4 * G + 4
                if u + LA < len(units):
                    qk(u + LA)
                s_t, s_ = sq.pop(u)
                p_t, p = pT_r.next()
                c.act(p[:], s_[:, :], AF.Exp, [s_t], [p_t], scale=0.125)
                for qi in range(4):
                    c.mm(po[qi][0], po[qi][1][:, 0:65], p[:, qi * 128:(qi + 1) * 128], va[:, kt, :],
                         kt == 0, kt == nkt - 1, [p_t, va_t])
                if kt == nkt - 1:
                    yo_t, yo = yo_r.next()
                    for qi in range(4):
                        r_t, r = rc_r.next()
                        c.op("dve", lambda e, r=r, pp=po[qi][1]: e.reciprocal(out=r[:], in_=pp[:, 64:65]),
                             [po[qi][0]], [r_t])
                        c.ts(yo[:, qi, :], po[qi][1][:, 0:64], r[:, 0:1], None, ALU.mult, None,
                             [po[qi][0], r_t], [yo_t])
                    c.dma(yc[b, G * 512:(G + 1) * 512, :].rearrange("(q p) d -> p q d", p=128), yo[:],
                          [yo_t], [], f"yo{(yo_r.i - 1) % 2}")
        c.end(["yo0", "yo1"])
    return nc


def rmsnorm_group(c, R, xg_t, xg, gam_t, gam, hT_t, hT, ntiles=4):
    idf_t, idf = R["idf"]
    stA, stB = {}, {}

    def A(n):
        st_t, stt_ = R["st"].next()
        c.op("dve", lambda e, o=stt_, x=xg, n=n: e.bn_stats(out=o[:, 0:6], in_=x[:, n, 0:512]), [xg_t], [st_t])
        c.op("dve", lambda e, o=stt_, x=xg, n=n: e.bn_stats(out=o[:, 6:12], in_=x[:, n, 512:1024]), [xg_t], [st_t])
        mv_t, mv = R["mv"].next()
        c.op("dve", lambda e, o=mv, i=stt_: e.bn_aggr(out=o[:, 0:2], in_=i[:, 0:12]), [st_t], [mv_t])
        ms_t, msq = R["ms"].next()
        c.stt(msq[:], mv[:, 0:1], mv[:, 0:1], mv[:, 1:2], ALU.mult, ALU.add, [mv_t], [ms_t])
        c.ts(msq[:], msq[:], EPS, None, ALU.add, None, [ms_t], [ms_t])
        rs_t, rs = R["rs"].next()
        c.act(rs[:], msq[:], AF.Sqrt, [ms_t], [rs_t])
        stA[n] = (rs_t, rs)

    def B(n):
        rs_t, rs = stA.pop(n)
        c.op("dve", lambda e, r=rs: e.reciprocal(out=r[:], in_=r[:]), [rs_t], [rs_t])
        hn_t, hn = R["hn"].next()
        c.stt(hn[:], xg[:, n, :], rs[:, 0:1], gam[:], ALU.mult, ALU.mult, [xg_t, rs_t, gam_t], [hn_t])
        banks = []
        for half in range(2):
            b_t, bk = c.ps()
            for j in range(4):
                kc = half * 4 + j
                c.tr(b_t, bk[:, j * 128:(j + 1) * 128], hn[:, kc * 128:(kc + 1) * 128], idf[:], [hn_t, idf_t])
            banks.append((b_t, bk))
        stB[n] = banks

    def C(n):
        for half, (b_t, bk) in enumerate(stB.pop(n)):
            c.cp(hT[:, half * 4:(half + 1) * 4, n * 128:(n + 1) * 128],
                 bk[:, :].rearrange("p (j t) -> p j t", t=128), [b_t], [hT_t],
                 eng=("act" if half == 0 else "dve"))

    for step in range(ntiles + 2):
        if step < ntiles:
            A(step)
        if 0 <= step - 1 < ntiles:
            B(step - 1)
        if 0 <= step - 2 < ntiles:
            C(step - 2)


def norm_rots(c, st, pfx, identf_d=None, hn_n=2):
    idf = None
    if identf_d is not None:
        idf = c.sb(st, pfx + "idf", [128, 128], F32)
        c.dma(idf[1][:], identf_d, [], [idf[0]], pfx + "idf")
    return dict(
        idf=idf,
        st=c.rot(st, pfx + "st", [128, 12], F32, 4),
        mv=c.rot(st, pfx + "mv", [128, 2], F32, 4),
        ms=c.rot(st, pfx + "ms", [128, 1], F32, 4),
        rs=c.rot(st, pfx + "rs", [128, 1], F32, 4),
        hn=c.rot(st, pfx + "hn", [128, 1024], F32, hn_n),
    )


def xview(x, g):
    return x[g * TG:(g + 1) * TG, :].rearrange("(n p) d -> p n d", p=128)


FG = 1024
NFG = NTOK // FG
FT = FG // 128


def stage_ffn(c, pfx, x_src, x_dst, xt_src, xt_dst, norm_d, wi, wo, identf_d):
    with contextlib.ExitStack() as st:
        c.begin()
        R = norm_rots(c, st, pfx, identf_d, hn_n=2)
        xg_r = c.rot(st, pfx + "xg", [128, FT, 1024], F32, 2)
        hT_t, hT = c.sb(st, pfx + "hT", [128, 8, FG], BF16)
        a_t, act = c.sb(st, pfx + "act", [128, 22, FG], BF16)
        wo_t, wo_sb = c.sb(st, pfx + "wo", [128, 22, 1024], BF16)
        wg_r = c.rot(st, pfx + "wg", [128, 8, 256], BF16, 2)
        wu_r = c.rot(st, pfx + "wu", [128, 8, 256], BF16, 2)
        sg_r = c.rot(st, pfx + "sg", [128, 512], F32, 2)
        gam_t, gam = c.sb(st, pfx + "gam", [128, 1024], F32)
        c.dma(gam[:], norm_d.to_broadcast([128, 1024]), [], [gam_t], pfx + "gam")

        def wload(j):
            g_t, g = wg_r.next()
            u_t, u = wu_r.next()
            k = (wg_r.i - 1) % 2
            c.dma(g[:], wi[:, j * 256:(j + 1) * 256].rearrange("(kc p) n -> p kc n", p=128), [], [g_t],
                  f"{pfx}wg{k}", q="pool")
            c.dma(u[:], wi[:, DFF + j * 256:DFF + (j + 1) * 256].rearrange("(kc p) n -> p kc n", p=128), [], [u_t],
                  f"{pfx}wu{k}", q="pool")
            return (g_t, g, u_t, u)

        nxt = wload(0)
        wo_pend = list(range(22))
        def xload(g):
            t_, b_ = xg_r.next()
            k_ = f"{pfx}xg{(xg_r.i - 1) % 2}"
            c.dma(b_[:], x_src[g * FG:(g + 1) * FG, :].rearrange("(n p) d -> p n d", p=128), [], [t_], k_)
            return t_, b_, k_

        nx = xload(0)
        for g in range(NFG):
            xg_t, xg, xk = nx
            if g + 1 < NFG:
                nx = xload(g + 1)
            rmsnorm_group(c, R, xg_t, xg, gam_t, gam, hT_t, hT, ntiles=FT)
            for j in range(11):
                g_t, gw, u_t, uw = nxt
                if not (g == NFG - 1 and j == 10):
                    nxt = wload((j + 1) % 11)
                for _ in range(2):
                    if wo_pend:
                        kc_ = wo_pend.pop(0)
                        c.dma(wo_sb[:, kc_, :], wo[kc_ * 128:(kc_ + 1) * 128, :], [], [wo_t], pfx + "wo", q="pool")
                for cc in range(2):
                    for hf in range(FG // 512):
                        ts_ = slice(hf * 512, (hf + 1) * 512)
                        pg_t, pg = c.ps()
                        pu_t, pu = c.ps()
                        for kc in range(8):
                            c.mm(pg_t, pg[:, :], gw[:, kc, cc * 128:(cc + 1) * 128], hT[:, kc, ts_], kc == 0, kc == 7,
                                 [g_t, hT_t])
                        for kc in range(8):
                            c.mm(pu_t, pu[:, :], uw[:, kc, cc * 128:(cc + 1) * 128], hT[:, kc, ts_], kc == 0, kc == 7,
                                 [u_t, hT_t])
                        sg_t, sg = sg_r.next()
                        c.act(sg[:], pg[:, :], AF.Silu, [pg_t], [sg_t])
                        c.tt(act[:, 2 * j + cc, ts_], sg[:], pu[:, :], ALU.mult, [sg_t, pu_t], [a_t])
            for n in range(FT):
                for half in range(2):
                    po_t, po = c.ps()
                    for kc in range(22):
                        c.mm(po_t, po[:, :], act[:, kc, n * 128:(n + 1) * 128],
                             wo_sb[:, kc, half * 512:(half + 1) * 512], kc == 0, kc == 21, [a_t, wo_t])
                    xs = xg[:, n, half * 512:(half + 1) * 512]
                    c.stt(xs, po[:, :], 0.5, xs, ALU.mult, ALU.add, [po_t, xg_t], [xg_t])
            c.dma(x_dst[g * FG:(g + 1) * FG, :].rearrange("(n p) d -> p n d", p=128), xg[:], [xg_t], [], xk)
        c.end([f"{pfx}xg{i}" for i in range(min(2, NFG))])


def stage_proj(c, pfx, x_src, xt_src, norm_d, w_qkv, w_glu, cosr, sinr, q_o, k_o, v_o, yglu_o, identf_d):
    with contextlib.ExitStack() as st:
        c.begin()
        R = norm_rots(c, st, pfx, identf_d, hn_n=4)
        xg_r = c.rot(st, pfx + "xg", [128, 4, 1024], F32, 2)
        hT_t, hT = c.sb(st, pfx + "hT", [128, 8, TG], BF16)
        wq_t, wq = c.sb(st, pfx + "wqkv", [128, 8, 1536], BF16)
        wc_t, wc = c.sb(st, pfx + "wglu", [128, 8, 1024], BF16)
        gam_t, gam = c.sb(st, pfx + "gam", [128, 1024], F32)
        cs_t, cs = c.sb(st, pfx + "cos", [128, 16, 64], F32)
        sn_t, sn = c.sb(st, pfx + "sin", [128, 16, 64], F32)
        qo_r = c.rot(st, pfx + "qo", [128, 3, 512], BF16, 2)
        t_r = c.rot(st, pfx + "rt", [128, 4, 64], F32, 2)
        sg_r = c.rot(st, pfx + "sg", [128, TG], F32, 2)
        yg_r = c.rot(st, pfx + "yg", [128, 4, TG], F32, 2)
        c.dma(gam[:], norm_d.to_broadcast([128, 1024]), [], [gam_t], pfx + "gam")
        c.dma(cs[:].rearrange("p n d -> p (n d)"), cosr, [], [cs_t], pfx + "cs")
        c.dma(sn[:].rearrange("p n d -> p (n d)"), sinr, [], [sn_t], pfx + "sn")
        for j in range(3):
            c.dma(wq[:, :, j * 512:(j + 1) * 512],
                  w_qkv[:, j * 512:(j + 1) * 512].rearrange("(kc p) n -> p kc n", p=128), [], [wq_t],
                  pfx + "wq", q="pool")
        for j in range(2):
            c.dma(wc[:, :, j * 512:(j + 1) * 512],
                  w_glu[:, j * 512:(j + 1) * 512].rearrange("(kc p) n -> p kc n", p=128), [], [wc_t],
                  pfx + "wc", q="pool")
        outs = [q_o, k_o, v_o]
        for g in range(NG):
            xg_t, xg = xg_r.next()
            xk = f"{pfx}xg{(xg_r.i - 1) % 2}"
            c.dma(xg[:], xview(x_src, g), [xt_src[g]], [xg_t], xk)
            rmsnorm_group(c, R, xg_t, xg, gam_t, gam, hT_t, hT)
            for n in range(4):
                tile = g * 4 + n
                qo_t, qo = qo_r.next()
                qk = f"{pfx}qo{(qo_r.i - 1) % 2}"
                for w in range(3):
                    p_t, p = c.ps()
                    for kc in range(8):
                        c.mm(p_t, p[:, :], hT[:, kc, n * 128:(n + 1) * 128], wq[:, kc, w * 512:(w + 1) * 512],
                             kc == 0, kc == 7, [hT_t, wq_t])
                    c.cp(qo[:, w, :], p[:, :], [p_t], [qo_t], eng="act")
                    if w < 2:
                        pv = p[:, :].rearrange("p (h d) -> p h d", d=64)
                        ov = qo[:, w, :].rearrange("p (h d) -> p h d", d=64)
                        x1, x2 = pv[:, :, 0:8], pv[:, :, 8:16]
                        cc_ = cs[:, tile, :].rearrange("p (h d) -> p h d", d=8)
                        ss_ = sn[:, tile, :].rearrange("p (h d) -> p h d", d=8)
                        t_t, t = t_r.next()
                        tv = [t[:, i, :].rearrange("p (h d) -> p h d", d=8) for i in range(4)]
                        c.tt(tv[0], x1, cc_, ALU.mult, [p_t, cs_t], [t_t])
                        c.tt(tv[1], x2, ss_, ALU.mult, [p_t, sn_t], [t_t])
                        c.tt(tv[2], x2, cc_, ALU.mult, [p_t, cs_t], [t_t])
                        c.tt(tv[3], x1, ss_, ALU.mult, [p_t, sn_t], [t_t])
                        c.tt(ov[:, :, 0:8], tv[0], tv[1], ALU.subtract, [t_t], [qo_t])
                        c.tt(ov[:, :, 8:16], tv[2], tv[3], ALU.add, [t_t], [qo_t])
                for w in range(3):
                    c.dma(outs[w][tile * 128:(tile + 1) * 128, :], qo[:, w, :], [qo_t], [], qk)
            yg_t, yg = yg_r.next()
            yk = f"{pfx}yg{(yg_r.i - 1) % 2}"
            for ch in range(4):
                pa_t, pa = c.ps()
                pg_t, pg = c.ps()
                for kc in range(8):
                    c.mm(pa_t, pa[:, :], wc[:, kc, ch * 128:(ch + 1) * 128], hT[:, kc, :], kc == 0, kc == 7,
                         [wc_t, hT_t])
                for kc in range(8):
                    c.mm(pg_t, pg[:, :], wc[:, kc, 512 + ch * 128:512 + (ch + 1) * 128], hT[:, kc, :], kc == 0,
                         kc == 7, [wc_t, hT_t])
                sg_t, sg = sg_r.next()
                c.act(sg[:], pg[:, :], AF.Sigmoid, [pg_t], [sg_t])
                c.tt(yg[:, ch, :], sg[:], pa[:, :], ALU.mult, [sg_t, pa_t], [yg_t])
            c.dma(yglu_o.rearrange("(c p) t -> p c t", p=128)[:, :, g * TG:(g + 1) * TG], yg[:], [yg_t], [], yk)
        c.end([f"{pfx}qo0", f"{pfx}qo1", f"{pfx}yg0", f"{pfx}yg1"])


def gelu_tanh(c, out, src_ap, src_t, out_t, tmp_r):
    if GELU_TANH_NATIVE:
        c.act(out, src_ap, AF.Gelu_apprx_tanh, [src_t], [out_t])
        return
    a_t, a = tmp_r.next()
    b_t, b = tmp_r.next()
    c.act(a[:], src_ap, AF.Square, [src_t], [a_t])
    c.ts(a[:], a[:], 0.044715, 1.0, ALU.mult, ALU.add, [a_t], [a_t])
    c.tt(b[:], a[:], src_ap, ALU.mult, [a_t, src_t], [b_t])
    c.act(a[:], b[:], AF.Sigmoid, [b_t], [a_t], scale=1.5957691216057308)
    c.tt(out, a[:], src_ap, ALU.mult, [a_t, src_t], [out_t])


def stage_mixer(c, pfx, x_src, x_dst, xt_src, xt_dst, norm_d, w_uv, w_gate, w_branch, w_out, ppd, ygh_d, ycT_d,
                sgu_wT_d, sgu_b_d, sgu_g_d, sgu_bb_d, tril_d, identf_d, cdiag_d):
    with contextlib.ExitStack() as st:
        c.begin()
        c.ps_n = 7
        R = norm_rots(c, st, pfx, identf_d)
        xg_r = c.rot(st, pfx + "xg", [128, 4, 1024], F32, 1)
        hT_t, hT = c.sb(st, pfx + "hT", [128, 8, TG], BF16)
        gam_t, gam = c.sb(st, pfx + "gam", [128, 1024], F32)
        pp_t, pp = c.sb(st, pfx + "pp", [128, NPP], F32)
        ygh_t, ygh = c.sb(st, pfx + "ygh", [128, 4, TG + 30], BF16)
        cdg_t, cdg = c.sb(st, pfx + "cdg", [128, 31, 128], BF16)
        acc_t, acc = c.sb(st, pfx + "acc", [128, 4, TG], F32)
        sq_t, sq = c.sb(st, pfx + "sq", [128, 4, TG], F32)
        mean_t, mean = c.sb(st, pfx + "mean", [128, TG], F32)
        var_t, var = c.sb(st, pfx + "var", [128, TG], F32)
        rstd_t, rstd = c.sb(st, pfx + "rstd", [128, TG], F32)
        tn_r = c.rot(st, pfx + "tn", [128, TG], F32, 2)
        ya_t, ya = c.sb(st, pfx + "ya", [128, 4, TG], BF16)
        yb_t, yb = c.sb(st, pfx + "yb", [128, 4, TG], BF16)
        yc_t, ycs = c.sb(st, pfx + "yc", [128, 4, TG], BF16)
        uT_t, uT = c.sb(st, pfx + "uT", [128, 4, TG], BF16)
        vg_r = c.rot(st, pfx + "vg", [128, 512], F32, 2)
        vl_r = c.rot(st, pfx + "vl", [128, 512], BF16, 2)
        bst_r = c.rot(st, pfx + "bst", [128, 6], F32, 3)
        bmv_r = c.rot(st, pfx + "bmv", [128, 2], F32, 3)
        gt_r = c.rot(st, pfx + "gt", [128, 512], F32, 4)
        wuv_t, wuv = c.sb(st, pfx + "wuv", [128, 8, 1024], BF16)
        wo_t, wo = c.sb(st, pfx + "wout", [128, 8, 1024], BF16)
        wgt_r = c.rot(st, pfx + "wgt", [128, 8, 512], BF16, 2)
        wb_r = c.rot(st, pfx + "wb", [128, 4, 512], BF16, 2)
        m_t, m = c.sb(st, pfx + "m", [128, 8, TG], F32)
        mg_t, mg = c.sb(st, pfx + "mg", [128, 8, TG], BF16)
        gs_r = c.rot(st, pfx + "gs", [128, TG], F32, 2)
        tp_r = c.rot(st, pfx + "tp", [128, TG], F32, 2)
        lg_t, lg = c.sb(st, pfx + "lng", [128, 512], F32)
        lb_t, lb = c.sb(st, pfx + "lnb", [128, 512], F32)
        bs_t, bs = c.sb(st, pfx + "bs", [128, 512], F32)
        wsf_t, wsf = c.sb(st, pfx + "wsf", [128, 4, 128], F32)
        trl_t, trl = c.sb(st, pfx + "tril", [128, 128], F32)
        ws_t, ws = c.sb(st, pfx + "ws", [128, 4, 128], BF16)
        on_t, ones = c.sb(st, pfx + "ones", [128, 128], F32)

        c.dma(gam[:], norm_d.to_broadcast([128, 1024]), [], [gam_t], pfx + "gam")
        c.dma(pp[:], ppd, [], [pp_t], pfx + "pp")
        c.dma(lg[:], sgu_g_d.to_broadcast([128, 512]), [], [lg_t], pfx + "lg")
        c.dma(lb[:], sgu_bb_d.to_broadcast([128, 512]), [], [lb_t], pfx + "lb")
        c.dma(bs[:], sgu_b_d.to_broadcast([128, 512]), [], [bs_t], pfx + "bs")
        c.dma(wsf[:], sgu_wT_d.rearrange("g s t -> s g t"), [], [wsf_t], pfx + "wsf")
        c.dma(trl[:], tril_d, [], [trl_t], pfx + "trl")
        for gi in range(4):
            c.tt(ws[:, gi, :], wsf[:, gi, :], trl[:], ALU.mult, [wsf_t, trl_t], [ws_t])
        c.memset(ones[:], 1.0 / 512.0, [on_t])
        for j in range(2):
            c.dma(wuv[:, :, j * 512:(j + 1) * 512],
                  w_uv[:, j * 512:(j + 1) * 512].rearrange("(kc p) n -> p kc n", p=128), [], [wuv_t],
                  pfx + "wuv", q="pool")
            c.dma(wo[:, :, j * 512:(j + 1) * 512],
                  w_out[:, j * 512:(j + 1) * 512].rearrange("(kc p) n -> p kc n", p=128), [], [wo_t],
                  pfx + "wout", q="pool")
        cw = lambda ch, k: pp[:, ch * 31 + k:ch * 31 + k + 1]

        def conv_ops(g):
            ops = [lambda: c.dma(ygh[:], ygh_d.rearrange("(c p) t -> p c t", p=128)[:, :, g * TG:(g + 1) * TG + 30],
                                 [], [ygh_t], pfx + "ygh", q="pool")]
            pc_t, pc = c.psum[7]
            for ch in range(4):
                ops.append(lambda ch=ch: c.dma(cdg[:, 0:16, :], cdiag_d[ch, :, 0:16 * 128].rearrange(
                    "p (k n) -> p k n", n=128), [], [cdg_t], pfx + "cdg", q="pool"))
                ops.append(lambda ch=ch: c.dma(cdg[:, 16:31, :], cdiag_d[ch, :, 16 * 128:31 * 128].rearrange(
                    "p (k n) -> p k n", n=128), [], [cdg_t], pfx + "cdg", q="pool"))
                for k in range(31):
                    ops.append(lambda ch=ch, k=k: c.mm(pc_t, pc[:, :], cdg[:, k, :], ygh[:, ch, k:k + TG], k == 0, k == 30,
                                                       [cdg_t, ygh_t]))
                ops.append(lambda ch=ch: c.act(acc[:, ch, :], pc[:, :], AF.Identity, [pc_t, pp_t], [acc_t],
                                               bias=pp[:, 124 + ch:125 + ch]))
            return ops

        def gload(i, hh):
            g_t, gw = wgt_r.next()
            b_t, bw = wb_r.next()
            k = (wgt_r.i - 1) % 2
            c0 = i * 1024 + hh * 512
            c.dma(gw[:], w_gate[:, c0:c0 + 512].rearrange("(kc p) n -> p kc n", p=128), [], [g_t],
                  f"{pfx}wgt{k}", q="pool")
            c.dma(bw[:], w_branch[i, :, hh * 512:(hh + 1) * 512].rearrange("(kc p) n -> p kc n", p=128), [], [b_t],
                  f"{pfx}wb{k}", q="pool")
            return (g_t, gw, b_t, bw)

        nxtw = gload(0, 0)
        for g in range(NG):
            xg_t, xg = xg_r.next()
            xk = f"{pfx}xg0"
            c.dma(xg[:], xview(x_src, g), [xt_src[g]], [xg_t], xk)
            c.dma(ycs[:], ycT_d.rearrange("(c p) t -> p c t", p=128)[:, :, g * TG:(g + 1) * TG], [], [yc_t],
                  pfx + "ycs")
            rmsnorm_group(c, R, xg_t, xg, gam_t, gam, hT_t, hT)
            if g == 0:
                for f in conv_ops(0):
                    f()
            c.act(sq[:], acc[:], AF.Square, [acc_t], [sq_t])
            pm_t, pm = c.ps()
            pe_t, pe2 = c.ps()
            for ch in range(4):
                c.mm(pm_t, pm[:, :], ones[:], acc[:, ch, :], ch == 0, ch == 3, [on_t, acc_t])
            for ch in range(4):
                c.mm(pe_t, pe2[:, :], ones[:], sq[:, ch, :], ch == 0, ch == 3, [on_t, sq_t])
            c.cp(mean[:], pm[:, :], [pm_t], [mean_t], eng="act")
            c.tt(var[:], mean[:], mean[:], ALU.mult, [mean_t], [var_t])
            c.tt(var[:], pe2[:, :], var[:], ALU.subtract, [pe_t, var_t], [var_t])
            c.ts(var[:], var[:], EPS, None, ALU.add, None, [var_t], [var_t])
            c.act(rstd[:], var[:], AF.Sqrt, [var_t], [rstd_t])
            c.op("dve", lambda e: e.reciprocal(out=rstd[:], in_=rstd[:]), [rstd_t], [rstd_t])
            for ch in range(4):
                tn_t, tn = tn_r.next()
                c.tt(tn[:], acc[:, ch, :], mean[:], ALU.subtract, [acc_t, mean_t], [tn_t])
                c.tt(tn[:], tn[:], rstd[:], ALU.mult, [tn_t, rstd_t], [tn_t])
                c.act(ya[:, ch, :], tn[:], AF.Silu, [tn_t, pp_t], [ya_t],
                      bias=pp[:, 132 + ch:133 + ch], scale=pp[:, 128 + ch:129 + ch])
            K1_ = 1.5957691216057308

            def gelu_p1(src_ap, src_t):
                a_t, a_ = gt_r.next()
                b_t, b_ = gt_r.next()
                c.act(a_[:], src_ap, AF.Square, [src_t], [a_t])
                c.ts(a_[:], a_[:], 0.044715, 1.0, ALU.mult, ALU.add, [a_t], [a_t])
                c.tt(b_[:], a_[:], src_ap, ALU.mult, [a_t, src_t], [b_t])
                c.act(a_[:], b_[:], AF.Sigmoid, [b_t], [a_t], scale=K1_)
                return a_t, a_

            uS = {}

            def uA(ch):
                pu_t, pu = c.ps()
                for kc in range(8):
                    c.mm(pu_t, pu[:, :], wuv[:, kc, ch * 128:(ch + 1) * 128], hT[:, kc, :], kc == 0, kc == 7,
                         [wuv_t, hT_t])
                uS[ch] = (pu_t, pu) + gelu_p1(pu[:, :], pu_t)

            def uB(ch):
                pu_t, pu, a_t, a_ = uS.pop(ch)
                c.tt(uT[:, ch, :], a_[:], pu[:, :], ALU.mult, [a_t, pu_t], [uT_t])

            for step in range(4 + 1):
                if step < 4:
                    uA(step)
                if step >= 1:
                    uB(step - 1)

            vA_, vB_, vC_ = {}, {}, {}

            def vA(n):
                pv_t, pv = c.ps()
                for kc in range(8):
                    c.mm(pv_t, pv[:, :], hT[:, kc, n * 128:(n + 1) * 128], wuv[:, kc, 512:1024], kc == 0, kc == 7,
                         [hT_t, wuv_t])
                vA_[n] = (pv_t, pv) + gelu_p1(pv[:, :], pv_t)

            def vB(n):
                pv_t, pv, a_t, a_ = vA_.pop(n)
                vg_t, vg = vg_r.next()
                c.tt(vg[:], a_[:], pv[:, :], ALU.mult, [a_t, pv_t], [vg_t])
                bst_t, bst = bst_r.next()
                c.op("dve", lambda e, o=bst, i=vg: e.bn_stats(out=o[:], in_=i[:]), [vg_t], [bst_t])
                bmv_t, bmv = bmv_r.next()
                c.op("dve", lambda e, o=bmv, i=bst: e.bn_aggr(out=o[:], in_=i[:]), [bst_t], [bmv_t])
                c.ts(bmv[:, 1:2], bmv[:, 1:2], EPS, None, ALU.add, None, [bmv_t], [bmv_t])
                c.act(bmv[:, 1:2], bmv[:, 1:2], AF.Sqrt, [bmv_t], [bmv_t])
                vB_[n] = (vg_t, vg, bmv_t, bmv)

            def vC(n):
                vg_t, vg, bmv_t, bmv = vB_.pop(n)
                c.op("dve", lambda e, r=bmv: e.reciprocal(out=r[:, 1:2], in_=r[:, 1:2]), [bmv_t], [bmv_t])
                c.ts(vg[:], vg[:], bmv[:, 0:1], bmv[:, 1:2], ALU.subtract, ALU.mult, [vg_t, bmv_t], [vg_t])
                c.tt(vg[:], vg[:], lg[:], ALU.mult, [vg_t, lg_t], [vg_t])
                vl_t, vl = vl_r.next()
                c.tt(vl[:], vg[:], lb[:], ALU.add, [vg_t, lb_t], [vl_t])
                px_t, px = c.ps()
                for gi in range(4):
                    c.mm(px_t, px[:, gi * 128:(gi + 1) * 128], vl[:, gi * 128:(gi + 1) * 128], ws[:, gi, :],
                         True, True, [vl_t, ws_t])
                vC_[n] = (px_t, px)

            def vD(n):
                px_t, px = vC_.pop(n)
                tp_t, tp = tp_r.next()
                c.tt(tp[:], px[:, :], bs[:], ALU.add, [px_t, bs_t], [tp_t])
                c.tt(yb[:, :, n * 128:(n + 1) * 128], tp[:].rearrange("p (g t) -> p g t", t=128),
                     uT[:, :, n * 128:(n + 1) * 128], ALU.mult, [tp_t, uT_t], [yb_t])

            for step in range(4 + 3):
                if step < 4:
                    vA(step)
                if 0 <= step - 1 < 4:
                    vB(step - 1)
                if 0 <= step - 2 < 4:
                    vC(step - 2)
                if 0 <= step - 3 < 4:
                    vD(step - 3)
            ys = [(ya_t, ya), (yb_t, yb), (yc_t, ycs)]
            pend = conv_ops(g + 1) if g + 1 < NG else []
            for i in range(3):
                y_t, y = ys[i]
                for hh in range(2):
                    wgt_t, wgt, wb_t, wb = nxtw
                    step = (g * 3 + i) * 2 + hh + 1
                    if step < NG * 6:
                        nxtw = gload((step // 2) % 3, step % 2)
                    for dq in range(4):
                        dc = hh * 4 + dq
                        pz_t, pz = c.ps()
                        ppj_t, ppj = c.ps()
                        for kc in range(8):
                            c.mm(pz_t, pz[:, :], wgt[:, kc, dq * 128:(dq + 1) * 128], hT[:, kc, :], kc == 0, kc == 7,
                                 [wgt_t, hT_t])
                        for kc in range(4):
                            c.mm(ppj_t, ppj[:, :], wb[:, kc, dq * 128:(dq + 1) * 128], y[:, kc, :], kc == 0, kc == 3,
                                 [wb_t, y_t])
                        gs_t, gs = gs_r.next()
                        c.act(gs[:], pz[:, :], AF.Sigmoid, [pz_t, pp_t], [gs_t],
                              bias=pp[:, 136 + i * 8 + dc:137 + i * 8 + dc])
                        if i == 0:
                            c.tt(m[:, dc, :], gs[:], ppj[:, :], ALU.mult, [gs_t, ppj_t], [m_t])
                        else:
                            tp_t, tp = tp_r.next()
                            c.tt(tp[:], gs[:], ppj[:, :], ALU.mult, [gs_t, ppj_t], [tp_t])
                            if i == 1:
                                c.tt(m[:, dc, :], m[:, dc, :], tp[:], ALU.add, [m_t, tp_t], [m_t])
                            else:
                                c.tt(mg[:, dc, :], m[:, dc, :], tp[:], ALU.add, [m_t, tp_t], [mg_t])
                        for _ in range(6):
                            if pend:
                                pend.pop(0)()
            while pend:
                pend.pop(0)()
            for n in range(4):
                for half in range(2):
                    po_t, po = c.ps()
                    for kc in range(8):
                        c.mm(po_t, po[:, :], mg[:, kc, n * 128:(n + 1) * 128],
                             wo[:, kc, half * 512:(half + 1) * 512], kc == 0, kc == 7, [mg_t, wo_t])
                    xs = xg[:, n, half * 512:(half + 1) * 512]
                    c.tt(xs, xs, po[:, :], ALU.add, [xg_t, po_t], [xg_t])
            c.dma(xview(x_dst, g), xg[:], [xg_t], [xt_dst[g]], xk)
        c.end([pfx + "xg0"])
        c.ps_n = 8


def stage_final(c, pfx, x_src, xt_src, norm_d, out_d):
    with contextlib.ExitStack() as st:
        c.begin()
        xg_r = c.rot(st, pfx + "xg", [128, 4, 1024], F32, 2)
        R = norm_rots(c, st, pfx)
        gam_t, gam = c.sb(st, pfx + "gam", [128, 1024], F32)
        c.dma(gam[:], norm_d.to_broadcast([128, 1024]), [], [gam_t], pfx + "gam")
        for g in range(NG):
            xg_t, xg = xg_r.next()
            xk = f"{pfx}xg{(xg_r.i - 1) % 2}"
            c.dma(xg[:], xview(x_src, g), [xt_src[g]], [xg_t], xk)
            for n in range(4):
                st_t, stt_ = R["st"].next()
                c.op("dve", lambda e, o=stt_, x=xg, n=n: e.bn_stats(out=o[:, 0:6], in_=x[:, n, 0:512]), [xg_t], [st_t])
                c.op("dve", lambda e, o=stt_, x=xg, n=n: e.bn_stats(out=o[:, 6:12], in_=x[:, n, 512:1024]), [xg_t],
                     [st_t])
                mv_t, mv = R["mv"].next()
                c.op("dve", lambda e, o=mv, i=stt_: e.bn_aggr(out=o[:, 0:2], in_=i[:, 0:12]), [st_t], [mv_t])
                ms_t, msq = R["ms"].next()
                c.stt(msq[:], mv[:, 0:1], mv[:, 0:1], mv[:, 1:2], ALU.mult, ALU.add, [mv_t], [ms_t])
                c.ts(msq[:], msq[:], EPS, None, ALU.add, None, [ms_t], [ms_t])
                rs_t, rs = R["rs"].next()
                c.act(rs[:], msq[:], AF.Sqrt, [ms_t], [rs_t])
                c.op("dve", lambda e, r=rs: e.reciprocal(out=r[:], in_=r[:]), [rs_t], [rs_t])
                c.stt(xg[:, n, :], xg[:, n, :], rs[:, 0:1], gam[:], ALU.mult, ALU.mult, [xg_t, rs_t, gam_t], [xg_t])
            c.dma(xview(out_d, g), xg[:], [xg_t], [], xk)
        c.end([f"{pfx}xg0", f"{pfx}xg1"])


def build_tok(do_mix, do_p1, do_final, dbg=None):
    nc = bass.Bass("TRN2", target_bir_lowering=False)
    x_in = din(nc, "x_in", [NTOK, D], F32)
    identf_d = din(nc, "identf", [128, 128], F32)
    if do_mix:
        a = dict(
            mix_norm=din(nc, "a_mix_norm", [1, D], F32), w_uv=din(nc, "a_w_uv", [D, 1024], F32),
            w_gate=din(nc, "a_w_gate", [D, 3072], F32), w_branch=din(nc, "a_w_branch", [3, 512, D], F32),
            w_out=din(nc, "a_w_out", [D, D], F32), pp=din(nc, "a_pp", [128, NPP], F32),
            ygh=din(nc, "a_ygh", [512, NTOK + 30], F32), ycT=din(nc, "a_ycT", [512, NTOK], BF16),
            sgu_wT=din(nc, "a_sgu_wT", [4, 128, 128], F32), sgu_b=din(nc, "a_sgu_b", [1, 512], F32),
            sgu_g=din(nc, "a_sgu_g", [1, 512], F32), sgu_bb=din(nc, "a_sgu_bb", [1, 512], F32),
            tril=din(nc, "tril", [128, 128], F32), cdiag=din(nc, "a_cdiag", [4, 128, 31 * 128], F32),
            ffn2_norm=din(nc, "a_ffn2_norm", [1, D], F32), ffn2_wi=din(nc, "a_ffn2_wi", [D, 2 * DFF], F32),
            ffn2_wo=din(nc, "a_ffn2_wo", [DFF, D], F32))
    if do_p1:
        b = dict(
            ffn1_norm=din(nc, "b_ffn1_norm", [1, D], F32), ffn1_wi=din(nc, "b_ffn1_wi", [D, 2 * DFF], F32),
            ffn1_wo=din(nc, "b_ffn1_wo", [DFF, D], F32), mix_norm=din(nc, "b_mix_norm", [1, D], F32),
            w_qkv=din(nc, "b_w_qkv", [D, 1536], F32), w_glu=din(nc, "b_w_glu", [D, 1024], F32),
            cosr=din(nc, "cosr", [128, 1024], F32), sinr=din(nc, "sinr", [128, 1024], F32))
        q_o = dout(nc, "q_o", [NTOK, 512], BF16)
        k_o = dout(nc, "k_o", [NTOK, 512], BF16)
        v_o = dout(nc, "v_o", [NTOK, 512], BF16)
        yglu_o = dout(nc, "yglu_o", [512, NTOK], F32)
    if do_final:
        fin = din(nc, "final_norm", [1, D], F32)
    x_out = dout(nc, "x_out", [NTOK, D], F32)
    if do_mix:
        xs1 = nc.dram_tensor("xs1", [NTOK, D], F32, kind="Internal").ap()
        xs2 = nc.dram_tensor("xs2", [NTOK, D], F32, kind="Internal").ap()
    fresh = lambda nm: [T(f"{nm}{g}") for g in range(NG)]
    with contextlib.ExitStack() as st:
        c = Ctx(nc, st)
        cur = x_in
        if do_mix:
            stage_mixer(c, "m_", cur, xs1, fresh("a"), fresh("b"), a["mix_norm"], a["w_uv"], a["w_gate"],
                        a["w_branch"], a["w_out"], a["pp"], a["ygh"], a["ycT"], a["sgu_wT"], a["sgu_b"],
                        a["sgu_g"], a["sgu_bb"], a["tril"], identf_d, a["cdiag"])
            stage_ffn(c, "f2_", xs1, xs2, fresh("a"), fresh("b"), a["ffn2_norm"], a["ffn2_wi"], a["ffn2_wo"],
                      identf_d)
            cur = xs2
        if do_p1:
            if dbg != "noffn":
                stage_ffn(c, "f1_", cur, x_out, fresh("a"), fresh("b"), b["ffn1_norm"], b["ffn1_wi"], b["ffn1_wo"],
                          identf_d)
            if dbg != "noproj":
                    stage_proj(c, "p_", (x_in if dbg == "noffn" else x_out), fresh("a"), b["mix_norm"], b["w_qkv"], b["w_glu"], b["cosr"], b["sinr"],
                           q_o, k_o, v_o, yglu_o, identf_d)
        if do_final:
            stage_final(c, "fn_", cur, fresh("a"), fin, x_out)
    return nc


_NC = {}


def _get(name, fn):
    if name not in _NC:
        _NC[name] = fn()
    return _NC[name]


def _bf(a):
    return np.ascontiguousarray(a).astype(ml_dtypes.bfloat16) if a.dtype != ml_dtypes.bfloat16 else np.ascontiguousarray(a)


def _consts():
    f = np.float32
    identf = np.eye(128, dtype=f)
    tril = (np.arange(128)[:, None] <= np.arange(128)[None, :]).astype(f)
    pos = np.arange(SEQ, dtype=f)
    inv = (np.float32(500000.0) ** (-(np.arange(0, 16, 2, dtype=f)) / np.float32(16))).astype(f)
    ang = (pos[:, None] * inv[None, :]).astype(f)
    cosr = np.tile(np.cos(ang).astype(f), (1, 8))
    sinr = np.tile(np.sin(ang).astype(f), (1, 8))
    kaug = np.zeros((32, SEQ), dtype=f)
    for j in range(32):
        kaug[j, j * 256:(j + 1) * 256] = 1.0
    qb = np.arange(64) // 2
    pastneg = np.where(np.arange(32)[None, :] < qb[:, None], 0.0, -1e30).astype(f).reshape(1, 2048)
    ownm = (np.arange(32)[None, :] == qb[:, None]).astype(f).reshape(1, 2048)
    cm = np.zeros((4, 128, 512), dtype=f)
    for i in range(4):
        kk = i * 128 + np.arange(128)[:, None]
        cm[i] = np.where(kk <= np.arange(512)[None, :], 0.0, NEG)
    return dict(identf=identf, tril=tril, cosr=cosr, sinr=sinr, kaug=_bf(kaug), pastneg=pastneg, ownm=ownm,
                cmask=_bf(cm))


def _conv_diag(conv_w):
    d = np.zeros((4, 128, 31, 128), dtype=np.float32)
    p = np.arange(128)
    for ch in range(4):
        d[ch, p, :, p] = conv_w[:, ch * 128:(ch + 1) * 128].T
    return d.reshape(4, 128, 31 * 128)


def _pack_pp(conv_w, conv_b, ln_g, ln_b, gate_b):
    pp = np.zeros((128, NPP), dtype=np.float32)
    pp[:, 0:124] = conv_w.reshape(31, 4, 128).transpose(2, 1, 0).reshape(128, 124)
    pp[:, 124:128] = conv_b.reshape(4, 128).T
    pp[:, 128:132] = ln_g.reshape(4, 128).T
    pp[:, 132:136] = ln_b.reshape(4, 128).T
    pp[:, 136:160] = gate_b.reshape(3, 8, 128).transpose(2, 0, 1).reshape(128, 24)
    return pp


def _run(nc, maps):
    res = run_bass_kernel_spmd(nc, maps, core_ids=list(range(8)))
    return res.results


def kernel(x, ffn1_norm, ffn1_wi, ffn1_wo, mix_norm, w_in, conv_w, conv_b, conv_ln_g, conv_ln_b, sgu_ln_g,
           sgu_ln_b, sgu_w, sgu_b, w_branch, gate_b, w_out, ffn2_norm, ffn2_wi, ffn2_wo, final_norm):
    A = lambda v: np.ascontiguousarray(np.asarray(v, dtype=np.float32))
    x = A(x)
    P = dict(ffn1_norm=A(ffn1_norm), ffn1_wi=A(ffn1_wi), ffn1_wo=A(ffn1_wo), mix_norm=A(mix_norm), w_in=A(w_in),
             conv_w=A(conv_w), conv_b=A(conv_b), conv_ln_g=A(conv_ln_g), conv_ln_b=A(conv_ln_b),
             sgu_ln_g=A(sgu_ln_g), sgu_ln_b=A(sgu_ln_b), sgu_w=A(sgu_w), sgu_b=A(sgu_b), w_branch=A(w_branch),
             gate_b=A(gate_b), w_out=A(w_out), ffn2_norm=A(ffn2_norm), ffn2_wi=A(ffn2_wi), ffn2_wo=A(ffn2_wo))
    final_norm = A(final_norm)
    C = _consts()
    xs = [x[c // 4, (c % 4) * NTOK:(c % 4 + 1) * NTOK, :] for c in range(8)]

    def p1_inputs(l):
        w = P["w_in"][l]
        return dict(b_ffn1_norm=P["ffn1_norm"][l][None], b_ffn1_wi=P["ffn1_wi"][l], b_ffn1_wo=P["ffn1_wo"][l],
                    b_mix_norm=P["mix_norm"][l][None], b_w_qkv=A(w[:, 2048:3584]), b_w_glu=A(w[:, 0:1024]))

    def mix_inputs(l):
        w = P["w_in"][l]
        return dict(a_mix_norm=P["mix_norm"][l][None], a_w_uv=A(w[:, 1024:2048]), a_w_gate=A(w[:, 3584:6656]),
                    a_w_branch=P["w_branch"][l], a_w_out=P["w_out"][l],
                    a_pp=_pack_pp(P["conv_w"][l], P["conv_b"][l], P["conv_ln_g"][l], P["conv_ln_b"][l],
                                  P["gate_b"][l]),
                    a_cdiag=_conv_diag(P["conv_w"][l]),
                    a_sgu_wT=A(P["sgu_w"][l].transpose(0, 2, 1)), a_sgu_b=P["sgu_b"][l].reshape(1, 512),
                    a_sgu_g=P["sgu_ln_g"][l][None], a_sgu_bb=P["sgu_ln_b"][l][None], tril=C["tril"],
                    a_ffn2_norm=P["ffn2_norm"][l][None], a_ffn2_wi=P["ffn2_wi"][l], a_ffn2_wo=P["ffn2_wo"][l])

    def rope_tabs(c):
        o = (c % 4) * NTOK
        pm = lambda t: A(t[o:o + NTOK].reshape(16, 128, 64).transpose(1, 0, 2).reshape(128, 1024))
        return dict(cosr=pm(C["cosr"]), sinr=pm(C["sinr"]))

    def attention(res):
        q = np.stack([np.concatenate([res[b * 4 + j]["q_o"] for j in range(4)], 0) for b in range(2)])
        k = np.stack([np.concatenate([res[b * 4 + j]["k_o"] for j in range(4)], 0) for b in range(2)])
        v = np.stack([np.concatenate([res[b * 4 + j]["v_o"] for j in range(4)], 0) for b in range(2)])
        maps = []
        for h in range(8):
            sl = slice(h * 64, (h + 1) * 64)
            maps.append(dict(qT=_bf(q[:, :, sl].transpose(0, 2, 1)), kT=_bf(k[:, :, sl].transpose(0, 2, 1)),
                             v=_bf(v[:, :, sl]), kaug=C["kaug"], pastneg=C["pastneg"], ownm=C["ownm"],
                             cmask=C["cmask"], identf=C["identf"]))
        r = _run(_get("att", build_att), maps)
        Y = np.concatenate([np.asarray(r[h]["yc"]) for h in range(8)], axis=2)
        G = [np.concatenate([np.asarray(res[b * 4 + j]["yglu_o"]) for j in range(4)], 1) for b in range(2)]
        ycT, ygh = [], []
        for c in range(8):
            b, o = c // 4, (c % 4) * NTOK
            ycT.append(_bf(Y[b, o:o + NTOK, :].T))
            gh = np.zeros((512, NTOK + 30), dtype=np.float32)
            gh[:, 30:] = G[b][:, o:o + NTOK]
            if o > 0:
                gh[:, :30] = G[b][:, o - 30:o]
            ygh.append(gh)
        return ycT, ygh

    maps = [dict(x_in=xs[c], identf=C["identf"], **p1_inputs(0), **rope_tabs(c)) for c in range(8)]
    res = _run(_get("k1", lambda: build_tok(False, True, False)), maps)
    xcur = [np.asarray(res[c]["x_out"]) for c in range(8)]
    ycT, ygh = attention(res)
    maps = [dict(x_in=xcur[c], identf=C["identf"], a_ygh=ygh[c], a_ycT=ycT[c], **mix_inputs(0), **p1_inputs(1),
                 **rope_tabs(c)) for c in range(8)]
    res = _run(_get("k2", lambda: build_tok(True, True, False)), maps)
    xcur = [np.asarray(res[c]["x_out"]) for c in range(8)]
    ycT, ygh = attention(res)
    maps = [dict(x_in=xcur[c], identf=C["identf"], a_ygh=ygh[c], a_ycT=ycT[c], final_norm=final_norm[None],
                 **mix_inputs(1)) for c in range(8)]
    res = _run(_get("k3", lambda: build_tok(True, False, True)), maps)
    out = np.stack([np.concatenate([np.asarray(res[b * 4 + j]["x_out"]) for j in range(4)], 0) for b in range(2)])
    return out.astype(np.float32)
```

```python
import contextlib
import numpy as np
import ml_dtypes
import concourse.bass as bass
import concourse.mybir as mybir
from concourse.bass_utils import run_bass_kernel_spmd

F32, BF16 = mybir.dt.float32, mybir.dt.bfloat16
AF = mybir.ActivationFunctionType
ALU = mybir.AluOpType
AX = mybir.AxisListType

D = 1024
DFF = 2816
SEQ = 8192
NTOK = 2048
TG = 512
NG = NTOK // TG
EPS = 1e-6
NPP = 160
NEG = -30000.0
GELU_TANH_NATIVE = False


class T:
    __slots__ = ("name", "w", "r", "psum")

    def __init__(self, name, psum=False):
        self.name, self.w, self.r, self.psum = name, None, {}, psum


class Sched:
    ENG = ("pe", "act", "dve", "pool", "sp")

    def __init__(self, nc):
        self.nc = nc
        self.ops = []
        self.keytot = {}

    def op(self, eng, fn, reads=(), writes=(), key=None):
        idx = len(self.ops)
        is_dma = key is not None
        deps = set()

        def add(d, kind):
            o = self.ops[d]
            if o["key"] is not None:
                deps.add(("k", o["key"], self.keytot[o["key"]]))
            else:
                if o["eng"] == eng and not is_dma:
                    if eng == "pe" or kind == "war":
                        return
                deps.add(("e", d))

        for t in reads:
            if t.w is not None:
                add(t.w, "raw")
            if t.psum:
                for rk_, d in t.r.items():
                    if rk_ != eng:
                        add(d, "raw")
        for t in writes:
            if t.w is not None:
                add(t.w, "waw")
            for d in t.r.values():
                add(d, "war")
        if is_dma:
            self.keytot[key] = self.keytot.get(key, 0) + 16
        self.ops.append(dict(eng=eng, fn=fn, deps=deps, key=key))
        rk = ("k", key) if is_dma else eng
        for t in reads:
            t.r[rk] = idx
        for t in writes:
            t.w = idx
            t.r = {}
        return idx

    def emit(self, stack, final_keys, tag=""):
        nc = self.nc
        need = set(d[1] for o in self.ops for d in o["deps"] if d[0] == "e")
        cnt = {e: 0 for e in self.ENG}
        ms = {}
        for i, o in enumerate(self.ops):
            if i in need:
                cnt[o["eng"]] += 1
                ms[i] = cnt[o["eng"]]
        assert max(cnt.values()) < 60000, cnt
        sem_e = {e: stack.enter_context(nc.semaphore(f"se_{tag}{e}")) for e in self.ENG}
        sem_k = {k: stack.enter_context(nc.semaphore(f"sk_{tag}{k}")) for k in self.keytot}
        per = {e: [] for e in self.ENG}
        for i, o in enumerate(self.ops):
            per[o["eng"]].append(i)
        ops = self.ops
        keytot = self.keytot

        def run(e, eo):
            seen_e = {f: 0 for f in self.ENG}
            seen_k = {}
            for i in per[e]:
                o = ops[i]
                for d in sorted(o["deps"], key=str):
                    if d[0] == "e":
                        f = ops[d[1]]["eng"]
                        v = ms[d[1]]
                        if seen_e[f] < v:
                            eo.wait_ge(sem_e[f], v)
                            seen_e[f] = v
                    else:
                        _, k, v = d
                        if seen_k.get(k, 0) < v:
                            eo.wait_ge(sem_k[k], v)
                            seen_k[k] = v
                ins = o["fn"](eo)
                if o["key"] is not None:
                    ins.then_inc(sem_k[o["key"]], 16)
                elif i in ms:
                    ins.then_inc(sem_e[e], 1)
            if e == "sp":
                for k in final_keys:
                    eo.wait_ge(sem_k[k], keytot[k])

        with nc.Block() as block:
            @block.tensor
            def _(eo):
                run("pe", eo)

            @block.scalar
            def _(eo):
                run("act", eo)

            @block.vector
            def _(eo):
                run("dve", eo)

            @block.gpsimd
            def _(eo):
                run("pool", eo)

            @block.sync
            def _(eo):
                run("sp", eo)


class Ctx:
    def __init__(self, nc, stack):
        self.nc, self.stack = nc, stack
        self.s = None
        self.psum = []
        for i in range(8):
            t = stack.enter_context(nc.psum_tensor(f"ps{i}", [128, 512], F32))
            self.psum.append((T(f"ps{i}", psum=True), t))
        self.pi = 0
        self.nstage = 0

    def begin(self):
        self.s = Sched(self.nc)
        for t, _ in self.psum:
            t.w, t.r = None, {}

    def end(self, final_keys):
        self.s.emit(self.stack, final_keys, tag=f"s{self.nstage}_")
        self.nstage += 1

    def ps(self):
        r = self.psum[self.pi % getattr(self, "ps_n", 8)]
        self.pi += 1
        return r

    def sb(self, stack, name, shape, dt):
        t = stack.enter_context(self.nc.sbuf_tensor(name, shape, dt))
        return T(name), t

    def rot(self, stack, name, shape, dt, n):
        return Rot([self.sb(stack, f"{name}{i}", shape, dt) for i in range(n)])

    def op(self, *a, **k):
        return self.s.op(*a, **k)

    def dma(self, out, in_, reads, writes, key, q="sp"):
        self.s.op(q, lambda e: e.dma_start(out=out, in_=in_), reads, writes, key=key)

    def mm(self, pt, out, lhsT, rhs, start, stop, reads):
        self.s.op("pe", lambda e: e.matmul(out=out, lhsT=lhsT, rhs=rhs, start=start, stop=stop),
                  reads, [pt])

    def tr(self, pt, out, in_, ident, reads):
        self.s.op("pe", lambda e: e.transpose(out=out, in_=in_, identity=ident), reads, [pt])

    def act(self, out, in_, func, reads, writes, bias=None, scale=None):
        kw = {}
        if bias is not None:
            kw["bias"] = bias
        if scale is not None:
            kw["scale"] = scale
        self.s.op("act", lambda e: e.activation(out=out, in_=in_, func=func, **kw), reads, writes)

    def tt(self, out, in0, in1, op, reads, writes, eng="dve"):
        self.s.op(eng, lambda e: e.tensor_tensor(out=out, in0=in0, in1=in1, op=op), reads, writes)

    def ts(self, out, in0, s1, s2, op0, op1, reads, writes, eng="dve"):
        if op1 is None:
            self.s.op(eng, lambda e: e.tensor_scalar(out=out, in0=in0, scalar1=s1, scalar2=None, op0=op0),
                      reads, writes)
        else:
            self.s.op(eng, lambda e: e.tensor_scalar(out=out, in0=in0, scalar1=s1, scalar2=s2, op0=op0, op1=op1),
                      reads, writes)

    def stt(self, out, in0, scalar, in1, op0, op1, reads, writes):
        self.s.op("dve", lambda e: e.scalar_tensor_tensor(out=out, in0=in0, scalar=scalar, in1=in1,
                                                          op0=op0, op1=op1), reads, writes)

    def cp(self, out, in_, reads, writes, eng="dve"):
        if eng == "act":
            self.s.op("act", lambda e: e.copy(out=out, in_=in_), reads, writes)
        else:
            self.s.op(eng, lambda e: e.tensor_copy(out=out, in_=in_), reads, writes)

    def memset(self, out, val, writes, eng="dve"):
        self.s.op(eng, lambda e: e.memset(out, val), [], writes)


class Rot:
    def __init__(self, items):
        self.items, self.i = items, 0

    def next(self):
        r = self.items[self.i % len(self.items)]
        self.i += 1
        return r


def din(nc, name, shape, dt):
    return nc.dram_tensor(name, list(shape), dt, kind="ExternalInput").ap()


def dout(nc, name, shape, dt):
    return nc.dram_tensor(name, list(shape), dt, kind="ExternalOutput").ap()


def build_att():
    nc = bass.Bass("TRN2", target_bir_lowering=False)
    qT = din(nc, "qT", [2, 64, SEQ], BF16)
    kT = din(nc, "kT", [2, 64, SEQ], BF16)
    vv = din(nc, "v", [2, SEQ, 64], BF16)
    kaug = din(nc, "kaug", [32, SEQ], BF16)
    pastneg = din(nc, "pastneg", [1, 2048], F32)
    ownm = din(nc, "ownm", [1, 2048], F32)
    cmask = din(nc, "cmask", [4, 128, 512], BF16)
    identf_d = din(nc, "identf", [128, 128], F32)
    yc = dout(nc, "yc", [2, SEQ, 64], BF16)
    with contextlib.ExitStack() as st:
        c = Ctx(nc, st)
        c.begin()
        QA = [c.sb(st, f"QA{i}", [96, SEQ], BF16) for i in range(2)]
        QB = [T(f"QAb{i}") for i in range(2)]
        KA = [c.sb(st, f"KA{i}", [96, SEQ], BF16) for i in range(2)]
        VA = [c.sb(st, f"VA{i}", [128, 64, 65], BF16) for i in range(2)]
        pn_t, pn = c.sb(st, "pastneg_sb", [128, 64, 32], F32)
        ow_t, ow = c.sb(st, "own_sb", [128, 64, 32], F32)
        cm_t, cm = c.sb(st, "cm_sb", [128, 4, 512], BF16)
        idf_t, idf = c.sb(st, "identf_sb", [128, 128], F32)
        idb_t, idb = c.sb(st, "identb_sb", [128, 128], BF16)
        kmf_t, kmf = c.sb(st, "kmf", [64, 32], F32)
        qf_t, qf = c.sb(st, "qf", [64, SEQ], F32)
        gm_r = c.rot(st, "gm", [128, 4, 32], F32, 3)
        m8_r = c.rot(st, "m8", [128, 4, 8], F32, 3)
        thr_r = c.rot(st, "thr", [128, 4], F32, 3)
        b2_r = c.rot(st, "b2", [128, 4, 32], F32, 3)
        b96_r = c.rot(st, "b96", [128, 4, 96], F32, 3)
        pT_r = c.rot(st, "pT", [128, 512], BF16, 4)
        rc_r = c.rot(st, "rc", [128, 1], F32, 4)
        yo_r = c.rot(st, "yo", [128, 4, 64], BF16, 2)

        c.dma(pn[:].rearrange("p a b -> p (a b)"), pastneg.to_broadcast([128, 2048]), [], [pn_t], "pn")
        c.dma(ow[:].rearrange("p a b -> p (a b)"), ownm.to_broadcast([128, 2048]), [], [ow_t], "ow")
        c.dma(cm[:], cmask.rearrange("i p q -> p i q"), [], [cm_t], "cm")
        c.dma(idf[:], identf_d, [], [idf_t], "idf")
        c.cp(idb[:], idf[:], [idf_t], [idb_t])
        for (bt, b96) in b96_r.items:
            c.memset(b96[:], 0.0, [bt])
        for i in range(2):
            c.dma(KA[i][1][64:96, :], kaug, [], [KA[i][0]], f"KA{i}")
            c.memset(VA[i][1][:, :, 64:65], 1.0, [VA[i][0]])

        def load(b):
            qa_t, qa = QA[b % 2]
            ka_t, ka = KA[b % 2]
            va_t, va = VA[b % 2]
            c.dma(qa[0:64, :], qT[b], [], [qa_t], f"QA{b % 2}")
            c.dma(ka[0:64, :], kT[b], [], [ka_t], f"KA{b % 2}")
            vsrc = vv[b].rearrange("(n p) d -> p n d", p=128)
            for j in range(8):
                c.dma(va[:, j * 8:(j + 1) * 8, 0:64], vsrc[:, j * 8:(j + 1) * 8, :], [], [va_t], f"VA{b % 2}")

        load(0)
        for b in range(2):
            qa_t, qa = QA[b % 2]
            qb_t = QB[b % 2]
            ka_t, ka = KA[b % 2]
            va_t, va = VA[b % 2]
            if b + 1 < 2:
                load(b + 1)
            for hq in range(4):
                c.cp(qf[:, hq * 2048:(hq + 1) * 2048], qa[0:64, hq * 2048:(hq + 1) * 2048], [qa_t], [qf_t],
                     eng=("pool" if hq % 2 == 0 else "dve"))
            c.op("dve", lambda e, ka=ka: e.tensor_reduce(
                out=kmf[:], in_=ka[0:64, :].rearrange("p (j k) -> p j k", k=256), axis=AX.X, op=ALU.add),
                [ka_t], [kmf_t])
            c.ts(kmf[:], kmf[:], 1.0 / 256.0, None, ALU.mult, None, [kmf_t], [kmf_t])
            stA, stB = {}, {}

            def gA(G_):
                pg_t, pg = c.ps()
                for j in range(4):
                    qt = 4 * G_ + j
                    c.mm(pg_t, pg[:, j * 32:(j + 1) * 32], qf[:, qt * 128:(qt + 1) * 128], kmf[:], True, True,
                         [qf_t, kmf_t])
                stA[G_] = (pg_t, pg)

            def gB(G_):
                pg_t, pg = stA.pop(G_)
                gm_t, gm = gm_r.next()
                c.tt(gm[:], pg[:, 0:128].rearrange("p (j n) -> p j n", n=32), pn[:, 4 * G_:4 * G_ + 4, :], ALU.add,
                     [pg_t, pn_t], [gm_t])
                m8_t, m8 = m8_r.next()
                for j in range(4):
                    c.op("dve", lambda e, m8=m8, gm=gm, j=j: e.max(out=m8[:, j, :], in_=gm[:, j, :]), [gm_t], [m8_t])
                th_t, th = thr_r.next()
                c.ts(th[:], m8[:, :, 2], -1e29, None, ALU.max, None, [m8_t], [th_t])
                b2_t, b2 = b2_r.next()
                for j in range(4):
                    c.stt(b2[:, j, :], gm[:, j, :], th[:, j:j + 1], ow[:, 4 * G_ + j, :], ALU.is_ge, ALU.add,
                          [gm_t, th_t, ow_t], [b2_t])
                b96_t, b96 = b96_r.next()
                c.ts(b96[:, :, 64:96], b2[:], -NEG, NEG, ALU.mult, ALU.add, [b2_t], [b96_t])
                stB[G_] = (b96_t, b96)

            def gC(G_):
                b96_t, b96 = stB.pop(G_)
                pt_t, pt = c.ps()
                for j in range(4):
                    c.tr(pt_t, pt[0:96, j * 128:(j + 1) * 128], b96[:, j, :], idf[:], [b96_t, idf_t])
                c.cp(qa[64:96, G_ * 512:(G_ + 1) * 512], pt[64:96, :], [pt_t], [qb_t], eng="act")

            for step in range(16 + 2):
                if step < 16:
                    gA(step)
                if 0 <= step - 1 < 16:
                    gB(step - 1)
                if 0 <= step - 2 < 16:
                    gC(step - 2)
            units = [(G, kt) for G in range(16) for kt in range(4 * G + 4)]
            po = [c.psum[4 + i] for i in range(4)]
            sq = {}

            def qk(u, ka=ka, qa=qa, ka_t=ka_t, qa_t=qa_t, qb_t=qb_t):
                G, kt = units[u]
                s_t, s_ = c.psum[u % 4]
                own = kt >= 4 * G
                c.mm(s_t, s_[:, :], ka[0:96, kt * 128:(kt + 1) * 128], qa[0:96, G * 512:(G + 1) * 512],
                     True, not own, [ka_t, qa_t, qb_t])
                if own:
                    c.mm(s_t, s_[:, :], idb[:], cm[:, kt - 4 * G, :], False, True, [idb_t, cm_t])
                sq[u] = (s_t, s_)

            LA = 3
            for u0 in range(LA):
                qk(u0)
            for u in range(len(units)):
                G, kt = units[u]
                nkt = 4 * G + 4
                if u + LA < len(units):
                    qk(u + LA)
                s_t, s_ = sq.pop(u)
                p_t, p = pT_r.next()
                c.act(p[:], s_[:, :], AF.Exp, [s_t], [p_t], scale=0.125)
                for qi in range(4):
                    c.mm(po[qi][0], po[qi][1][:, 0:65], p[:, qi * 128:(qi + 1) * 128], va[:, kt, :],
                         kt == 0, kt == nkt - 1, [p_t, va_t])
                if kt == nkt - 1:
                    yo_t, yo = yo_r.next()
                    for qi in range(4):
                        r_t, r = rc_r.next()
                        c.op("dve", lambda e, r=r, pp=po[qi][1]: e.reciprocal(out=r[:], in_=pp[:, 64:65]),
                             [po[qi][0]], [r_t])
                        c.ts(yo[:, qi, :], po[qi][1][:, 0:64], r[:, 0:1], None, ALU.mult, None,
                             [po[qi][0], r_t], [yo_t])
                    c.dma(yc[b, G * 512:(G + 1) * 512, :].rearrange("(q p) d -> p q d", p=128), yo[:],
                          [yo_t], [], f"yo{(yo_r.i - 1) % 2}")
        c.end(["yo0", "yo1"])
    return nc


def rmsnorm_group(c, R, xg_t, xg, gam_t, gam, hT_t, hT, ntiles=4):
    idf_t, idf = R["idf"]
    stA, stB = {}, {}

    def A(n):
        st_t, stt_ = R["st"].next()
        c.op("dve", lambda e, o=stt_, x=xg, n=n: e.bn_stats(out=o[:, 0:6], in_=x[:, n, 0:512]), [xg_t], [st_t])
        c.op("dve", lambda e, o=stt_, x=xg, n=n: e.bn_stats(out=o[:, 6:12], in_=x[:, n, 512:1024]), [xg_t], [st_t])
        mv_t, mv = R["mv"].next()
        c.op("dve", lambda e, o=mv, i=stt_: e.bn_aggr(out=o[:, 0:2], in_=i[:, 0:12]), [st_t], [mv_t])
        ms_t, msq = R["ms"].next()
        c.stt(msq[:], mv[:, 0:1], mv[:, 0:1], mv[:, 1:2], ALU.mult, ALU.add, [mv_t], [ms_t])
        c.ts(msq[:], msq[:], EPS, None, ALU.add, None, [ms_t], [ms_t])
        rs_t, rs = R["rs"].next()
        c.act(rs[:], msq[:], AF.Sqrt, [ms_t], [rs_t])
        stA[n] = (rs_t, rs)

    def B(n):
        rs_t, rs = stA.pop(n)
        c.op("dve", lambda e, r=rs: e.reciprocal(out=r[:], in_=r[:]), [rs_t], [rs_t])
        hn_t, hn = R["hn"].next()
        c.stt(hn[:], xg[:, n, :], rs[:, 0:1], gam[:], ALU.mult, ALU.mult, [xg_t, rs_t, gam_t], [hn_t])
        banks = []
        for half in range(2):
            b_t, bk = c.ps()
            for j in range(4):
                kc = half * 4 + j
                c.tr(b_t, bk[:, j * 128:(j + 1) * 128], hn[:, kc * 128:(kc + 1) * 128], idf[:], [hn_t, idf_t])
            banks.append((b_t, bk))
        stB[n] = banks

    def C(n):
        ht = hT_t[n // 4] if isinstance(hT_t, list) else hT_t
        for half, (b_t, bk) in enumerate(stB.pop(n)):
            c.cp(hT[:, half * 4:(half + 1) * 4, n * 128:(n + 1) * 128],
                 bk[:, :].rearrange("p (j t) -> p j t", t=128), [b_t], [ht],
                 eng=("act" if half == 0 else "dve"))

    for step in range(ntiles + 2):
        if step < ntiles:
            A(step)
        if 0 <= step - 1 < ntiles:
            B(step - 1)
        if 0 <= step - 2 < ntiles:
            C(step - 2)


def norm_rots(c, st, pfx, identf_d=None, hn_n=2):
    idf = None
    if identf_d is not None:
        idf = c.sb(st, pfx + "idf", [128, 128], F32)
        c.dma(idf[1][:], identf_d, [], [idf[0]], pfx + "idf")
    return dict(
        idf=idf,
        st=c.rot(st, pfx + "st", [128, 12], F32, 4),
        mv=c.rot(st, pfx + "mv", [128, 2], F32, 4),
        ms=c.rot(st, pfx + "ms", [128, 1], F32, 4),
        rs=c.rot(st, pfx + "rs", [128, 1], F32, 4),
        hn=c.rot(st, pfx + "hn", [128, 1024], F32, hn_n),
    )


def xview(x, g):
    return x[g * TG:(g + 1) * TG, :].rearrange("(n p) d -> p n d", p=128)


FG = 1024
NFG = NTOK // FG
FT = FG // 128


def stage_ffn(c, pfx, x_src, x_dst, xt_src, xt_dst, norm_d, wi, wo, identf_d):
    with contextlib.ExitStack() as st:
        c.begin()
        R = norm_rots(c, st, pfx, identf_d, hn_n=2)
        xg_r = c.rot(st, pfx + "xg", [128, FT, 1024], F32, 2)
        _, hT = c.sb(st, pfx + "hT", [128, 8, FG], BF16)
        hT_t = [T(f"{pfx}hT_h{i}") for i in range(FG // 512)]
        a_t, act = c.sb(st, pfx + "act", [128, 22, FG], BF16)
        wo_t, wo_sb = c.sb(st, pfx + "wo", [128, 22, 1024], BF16)
        wg_r = c.rot(st, pfx + "wg", [128, 8, 256], BF16, 2)
        wu_r = c.rot(st, pfx + "wu", [128, 8, 256], BF16, 2)
        sg_r = c.rot(st, pfx + "sg", [128, 512], F32, 2)
        gam_t, gam = c.sb(st, pfx + "gam", [128, 1024], F32)
        c.dma(gam[:], norm_d.to_broadcast([128, 1024]), [], [gam_t], pfx + "gam")

        def wload(j):
            g_t, g = wg_r.next()
            u_t, u = wu_r.next()
            k = (wg_r.i - 1) % 2
            c.dma(g[:], wi[:, j * 256:(j + 1) * 256].rearrange("(kc p) n -> p kc n", p=128), [], [g_t],
                  f"{pfx}wg{k}", q="pool")
            c.dma(u[:], wi[:, DFF + j * 256:DFF + (j + 1) * 256].rearrange("(kc p) n -> p kc n", p=128), [], [u_t],
                  f"{pfx}wu{k}", q="pool")
            return (g_t, g, u_t, u)

        nxt = wload(0)
        wo_pend = list(range(22))
        def xload(g):
            t_, b_ = xg_r.next()
            k_ = f"{pfx}xg{(xg_r.i - 1) % 2}"
            c.dma(b_[:], x_src[g * FG:(g + 1) * FG, :].rearrange("(n p) d -> p n d", p=128), [], [t_], k_)
            return t_, b_, k_

        nx = xload(0)
        for g in range(NFG):
            xg_t, xg, xk = nx
            if g + 1 < NFG:
                nx = xload(g + 1)
            rmsnorm_group(c, R, xg_t, xg, gam_t, gam, hT_t, hT, ntiles=FT)
            for j in range(11):
                g_t, gw, u_t, uw = nxt
                if not (g == NFG - 1 and j == 10):
                    nxt = wload((j + 1) % 11)
                for _ in range(2):
                    if wo_pend:
                        kc_ = wo_pend.pop(0)
                        c.dma(wo_sb[:, kc_, :], wo[kc_ * 128:(kc_ + 1) * 128, :], [], [wo_t], pfx + "wo", q="pool")
                for cc in range(2):
                    for hf in range(FG // 512):
                        ts_ = slice(hf * 512, (hf + 1) * 512)
                        pg_t, pg = c.ps()
                        pu_t, pu = c.ps()
                        for kc in range(8):
                            c.mm(pg_t, pg[:, :], gw[:, kc, cc * 128:(cc + 1) * 128], hT[:, kc, ts_], kc == 0, kc == 7,
                                 [g_t, hT_t[hf]])
                        for kc in range(8):
                            c.mm(pu_t, pu[:, :], uw[:, kc, cc * 128:(cc + 1) * 128], hT[:, kc, ts_], kc == 0, kc == 7,
                                 [u_t, hT_t[hf]])
                        sg_t, sg = sg_r.next()
                        c.act(sg[:], pg[:, :], AF.Silu, [pg_t], [sg_t])
                        c.tt(act[:, 2 * j + cc, ts_], sg[:], pu[:, :], ALU.mult, [sg_t, pu_t], [a_t])
            for n in range(FT):
                for half in range(2):
                    po_t, po = c.ps()
                    for kc in range(22):
                        c.mm(po_t, po[:, :], act[:, kc, n * 128:(n + 1) * 128],
                             wo_sb[:, kc, half * 512:(half + 1) * 512], kc == 0, kc == 21, [a_t, wo_t])
                    xs = xg[:, n, half * 512:(half + 1) * 512]
                    c.stt(xs, po[:, :], 0.5, xs, ALU.mult, ALU.add, [po_t, xg_t], [xg_t])
            c.dma(x_dst[g * FG:(g + 1) * FG, :].rearrange("(n p) d -> p n d", p=128), xg[:], [xg_t], [], xk)
        c.end([f"{pfx}xg{i}" for i in range(min(2, NFG))])


def stage_proj(c, pfx, x_src, xt_src, norm_d, w_qkv, w_glu, cosr, sinr, q_o, k_o, v_o, yglu_o, identf_d):
    with contextlib.ExitStack() as st:
        c.begin()
        R = norm_rots(c, st, pfx, identf_d, hn_n=4)
        xg_r = c.rot(st, pfx + "xg", [128, 4, 1024], F32, 2)
        hT_t, hT = c.sb(st, pfx + "hT", [128, 8, TG], BF16)
        wq_t, wq = c.sb(st, pfx + "wqkv", [128, 8, 1536], BF16)
        wc_t, wc = c.sb(st, pfx + "wglu", [128, 8, 1024], BF16)
        gam_t, gam = c.sb(st, pfx + "gam", [128, 1024], F32)
        cs_t, cs = c.sb(st, pfx + "cos", [128, 16, 64], F32)
        sn_t, sn = c.sb(st, pfx + "sin", [128, 16, 64], F32)
        qo_r = c.rot(st, pfx + "qo", [128, 3, 512], BF16, 2)
        t_r = c.rot(st, pfx + "rt", [128, 4, 64], F32, 2)
        sg_r = c.rot(st, pfx + "sg", [128, TG], F32, 2)
        yg_r = c.rot(st, pfx + "yg", [128, 4, TG], F32, 2)
        c.dma(gam[:], norm_d.to_broadcast([128, 1024]), [], [gam_t], pfx + "gam")
        c.dma(cs[:].rearrange("p n d -> p (n d)"), cosr, [], [cs_t], pfx + "cs")
        c.dma(sn[:].rearrange("p n d -> p (n d)"), sinr, [], [sn_t], pfx + "sn")
        for j in range(3):
            c.dma(wq[:, :, j * 512:(j + 1) * 512],
                  w_qkv[:, j * 512:(j + 1) * 512].rearrange("(kc p) n -> p kc n", p=128), [], [wq_t],
                  pfx + "wq", q="pool")
        for j in range(2):
            c.dma(wc[:, :, j * 512:(j + 1) * 512],
                  w_glu[:, j * 512:(j + 1) * 512].rearrange("(kc p) n -> p kc n", p=128), [], [wc_t],
                  pfx + "wc", q="pool")
        outs = [q_o, k_o, v_o]
        for g in range(NG):
            xg_t, xg = xg_r.next()
            xk = f"{pfx}xg{(xg_r.i - 1) % 2}"
            c.dma(xg[:], xview(x_src, g), [xt_src[g]], [xg_t], xk)
            rmsnorm_group(c, R, xg_t, xg, gam_t, gam, hT_t, hT)
            for n in range(4):
                tile = g * 4 + n
                qo_t, qo = qo_r.next()
                qk = f"{pfx}qo{(qo_r.i - 1) % 2}"
                for w in range(3):
                    p_t, p = c.ps()
                    for kc in range(8):
                        c.mm(p_t, p[:, :], hT[:, kc, n * 128:(n + 1) * 128], wq[:, kc, w * 512:(w + 1) * 512],
                             kc == 0, kc == 7, [hT_t, wq_t])
                    c.cp(qo[:, w, :], p[:, :], [p_t], [qo_t], eng="act")
                    if w < 2:
                        pv = p[:, :].rearrange("p (h d) -> p h d", d=64)
                        ov = qo[:, w, :].rearrange("p (h d) -> p h d", d=64)
                        x1, x2 = pv[:, :, 0:8], pv[:, :, 8:16]
                        cc_ = cs[:, tile, :].rearrange("p (h d) -> p h d", d=8)
                        ss_ = sn[:, tile, :].rearrange("p (h d) -> p h d", d=8)
                        t_t, t = t_r.next()
                        tv = [t[:, i, :].rearrange("p (h d) -> p h d", d=8) for i in range(4)]
                        c.tt(tv[0], x1, cc_, ALU.mult, [p_t, cs_t], [t_t])
                        c.tt(tv[1], x2, ss_, ALU.mult, [p_t, sn_t], [t_t])
                        c.tt(tv[2], x2, cc_, ALU.mult, [p_t, cs_t], [t_t])
                        c.tt(tv[3], x1, ss_, ALU.mult, [p_t, sn_t], [t_t])
                        c.tt(ov[:, :, 0:8], tv[0], tv[1], ALU.subtract, [t_t], [qo_t])
                        c.tt(ov[:, :, 8:16], tv[2], tv[3], ALU.add, [t_t], [qo_t])
                for w in range(3):
                    c.dma(outs[w][tile * 128:(tile + 1) * 128, :], qo[:, w, :], [qo_t], [], qk)
            yg_t, yg = yg_r.next()
            yk = f"{pfx}yg{(yg_r.i - 1) % 2}"
            for ch in range(4):
                pa_t, pa = c.ps()
                pg_t, pg = c.ps()
                for kc in range(8):
                    c.mm(pa_t, pa[:, :], wc[:, kc, ch * 128:(ch + 1) * 128], hT[:, kc, :], kc == 0, kc == 7,
                         [wc_t, hT_t])
                for kc in range(8):
                    c.mm(pg_t, pg[:, :], wc[:, kc, 512 + ch * 128:512 + (ch + 1) * 128], hT[:, kc, :], kc == 0,
                         kc == 7, [wc_t, hT_t])
                sg_t, sg = sg_r.next()
                c.act(sg[:], pg[:, :], AF.Sigmoid, [pg_t], [sg_t])
                c.tt(yg[:, ch, :], sg[:], pa[:, :], ALU.mult, [sg_t, pa_t], [yg_t])
            c.dma(yglu_o.rearrange("(c p) t -> p c t", p=128)[:, :, g * TG:(g + 1) * TG], yg[:], [yg_t], [], yk)
        c.end([f"{pfx}qo0", f"{pfx}qo1", f"{pfx}yg0", f"{pfx}yg1"])


def gelu_tanh(c, out, src_ap, src_t, out_t, tmp_r):
    if GELU_TANH_NATIVE:
        c.act(out, src_ap, AF.Gelu_apprx_tanh, [src_t], [out_t])
        return
    a_t, a = tmp_r.next()
    b_t, b = tmp_r.next()
    c.act(a[:], src_ap, AF.Square, [src_t], [a_t])
    c.ts(a[:], a[:], 0.044715, 1.0, ALU.mult, ALU.add, [a_t], [a_t])
    c.tt(b[:], a[:], src_ap, ALU.mult, [a_t, src_t], [b_t])
    c.act(a[:], b[:], AF.Sigmoid, [b_t], [a_t], scale=1.5957691216057308)
    c.tt(out, a[:], src_ap, ALU.mult, [a_t, src_t], [out_t])


def stage_mixer(c, pfx, x_src, x_dst, xt_src, xt_dst, norm_d, w_uv, w_gate, w_branch, w_out, ppd, ygh_d, ycT_d,
                sgu_wT_d, sgu_b_d, sgu_g_d, sgu_bb_d, tril_d, identf_d, cdiag_d):
    with contextlib.ExitStack() as st:
        c.begin()
        c.ps_n = 7
        R = norm_rots(c, st, pfx, identf_d)
        xg_r = c.rot(st, pfx + "xg", [128, 4, 1024], F32, 1)
        hT_t, hT = c.sb(st, pfx + "hT", [128, 8, TG], BF16)
        gam_t, gam = c.sb(st, pfx + "gam", [128, 1024], F32)
        pp_t, pp = c.sb(st, pfx + "pp", [128, NPP], F32)
        ygh_t, ygh = c.sb(st, pfx + "ygh", [128, 4, TG + 30], BF16)
        cdg_t, cdg = c.sb(st, pfx + "cdg", [128, 31, 128], BF16)
        acc_t, acc = c.sb(st, pfx + "acc", [128, 4, TG], F32)
        sq_t, sq = c.sb(st, pfx + "sq", [128, 4, TG], F32)
        mean_t, mean = c.sb(st, pfx + "mean", [128, TG], F32)
        var_t, var = c.sb(st, pfx + "var", [128, TG], F32)
        rstd_t, rstd = c.sb(st, pfx + "rstd", [128, TG], F32)
        tn_r = c.rot(st, pfx + "tn", [128, TG], F32, 2)
        ya_t, ya = c.sb(st, pfx + "ya", [128, 4, TG], BF16)
        yb_t, yb = c.sb(st, pfx + "yb", [128, 4, TG], BF16)
        yc_t, ycs = c.sb(st, pfx + "yc", [128, 4, TG], BF16)
        uT_t, uT = c.sb(st, pfx + "uT", [128, 4, TG], BF16)
        vg_r = c.rot(st, pfx + "vg", [128, 512], F32, 2)
        vl_r = c.rot(st, pfx + "vl", [128, 512], BF16, 2)
        bst_r = c.rot(st, pfx + "bst", [128, 6], F32, 3)
        bmv_r = c.rot(st, pfx + "bmv", [128, 2], F32, 3)
        gt_r = c.rot(st, pfx + "gt", [128, 512], F32, 4)
        wuv_t, wuv = c.sb(st, pfx + "wuv", [128, 8, 1024], BF16)
        wo_t, wo = c.sb(st, pfx + "wout", [128, 8, 1024], BF16)
        wgt_r = c.rot(st, pfx + "wgt", [128, 8, 512], BF16, 2)
        wb_r = c.rot(st, pfx + "wb", [128, 4, 512], BF16, 2)
        m_t, m = c.sb(st, pfx + "m", [128, 8, TG], F32)
        mg_t, mg = c.sb(st, pfx + "mg", [128, 8, TG], BF16)
        gs_r = c.rot(st, pfx + "gs", [128, TG], F32, 2)
        tp_r = c.rot(st, pfx + "tp", [128, TG], F32, 2)
        lg_t, lg = c.sb(st, pfx + "lng", [128, 512], F32)
        lb_t, lb = c.sb(st, pfx + "lnb", [128, 512], F32)
        bs_t, bs = c.sb(st, pfx + "bs", [128, 512], F32)
        wsf_t, wsf = c.sb(st, pfx + "wsf", [128, 4, 128], F32)
        trl_t, trl = c.sb(st, pfx + "tril", [128, 128], F32)
        ws_t, ws = c.sb(st, pfx + "ws", [128, 4, 128], BF16)
        on_t, ones = c.sb(st, pfx + "ones", [128, 128], F32)

        c.dma(gam[:], norm_d.to_broadcast([128, 1024]), [], [gam_t], pfx + "gam")
        c.dma(pp[:], ppd, [], [pp_t], pfx + "pp")
        c.dma(lg[:], sgu_g_d.to_broadcast([128, 512]), [], [lg_t], pfx + "lg")
        c.dma(lb[:], sgu_bb_d.to_broadcast([128, 512]), [], [lb_t], pfx + "lb")
        c.dma(bs[:], sgu_b_d.to_broadcast([128, 512]), [], [bs_t], pfx + "bs")
        c.dma(wsf[:], sgu_wT_d.rearrange("g s t -> s g t"), [], [wsf_t], pfx + "wsf")
        c.dma(trl[:], tril_d, [], [trl_t], pfx + "trl")
        for gi in range(4):
            c.tt(ws[:, gi, :], wsf[:, gi, :], trl[:], ALU.mult, [wsf_t, trl_t], [ws_t])
        c.memset(ones[:], 1.0 / 512.0, [on_t])
        for j in range(2):
            c.dma(wuv[:, :, j * 512:(j + 1) * 512],
                  w_uv[:, j * 512:(j + 1) * 512].rearrange("(kc p) n -> p kc n", p=128), [], [wuv_t],
                  pfx + "wuv", q="pool")
            c.dma(wo[:, :, j * 512:(j + 1) * 512],
                  w_out[:, j * 512:(j + 1) * 512].rearrange("(kc p) n -> p kc n", p=128), [], [wo_t],
                  pfx + "wout", q="pool")
        cw = lambda ch, k: pp[:, ch * 31 + k:ch * 31 + k + 1]

        def conv_ops(g):
            ops = [lambda: c.dma(ygh[:], ygh_d.rearrange("(c p) t -> p c t", p=128)[:, :, g * TG:(g + 1) * TG + 30],
                                 [], [ygh_t], pfx + "ygh", q="pool")]
            pc_t, pc = c.psum[7]
            for ch in range(4):
                ops.append(lambda ch=ch: c.dma(cdg[:, 0:16, :], cdiag_d[ch, :, 0:16 * 128].rearrange(
                    "p (k n) -> p k n", n=128), [], [cdg_t], pfx + "cdg", q="pool"))
                ops.append(lambda ch=ch: c.dma(cdg[:, 16:31, :], cdiag_d[ch, :, 16 * 128:31 * 128].rearrange(
                    "p (k n) -> p k n", n=128), [], [cdg_t], pfx + "cdg", q="pool"))
                for k in range(31):
                    ops.append(lambda ch=ch, k=k: c.mm(pc_t, pc[:, :], cdg[:, k, :], ygh[:, ch, k:k + TG], k == 0, k == 30,
                                                       [cdg_t, ygh_t]))
                ops.append(lambda ch=ch: c.act(acc[:, ch, :], pc[:, :], AF.Identity, [pc_t, pp_t], [acc_t],
                                               bias=pp[:, 124 + ch:125 + ch]))
            return ops

        def gload(i, hh):
            g_t, gw = wgt_r.next()
            b_t, bw = wb_r.next()
            k = (wgt_r.i - 1) % 2
            c0 = i * 1024 + hh * 512
            c.dma(gw[:], w_gate[:, c0:c0 + 512].rearrange("(kc p) n -> p kc n", p=128), [], [g_t],
                  f"{pfx}wgt{k}", q="pool")
            c.dma(bw[:], w_branch[i, :, hh * 512:(hh + 1) * 512].rearrange("(kc p) n -> p kc n", p=128), [], [b_t],
                  f"{pfx}wb{k}", q="pool")
            return (g_t, gw, b_t, bw)

        nxtw = gload(0, 0)
        for g in range(NG):
            xg_t, xg = xg_r.next()
            xk = f"{pfx}xg0"
            c.dma(xg[:], xview(x_src, g), [xt_src[g]], [xg_t], xk)
            c.dma(ycs[:], ycT_d.rearrange("(c p) t -> p c t", p=128)[:, :, g * TG:(g + 1) * TG], [], [yc_t],
                  pfx + "ycs")
            rmsnorm_group(c, R, xg_t, xg, gam_t, gam, hT_t, hT)
            if g == 0:
                for f in conv_ops(0):
                    f()
            c.act(sq[:], acc[:], AF.Square, [acc_t], [sq_t])
            pm_t, pm = c.ps()
            pe_t, pe2 = c.ps()
            for ch in range(4):
                c.mm(pm_t, pm[:, :], ones[:], acc[:, ch, :], ch == 0, ch == 3, [on_t, acc_t])
            for ch in range(4):
                c.mm(pe_t, pe2[:, :], ones[:], sq[:, ch, :], ch == 0, ch == 3, [on_t, sq_t])
            c.cp(mean[:], pm[:, :], [pm_t], [mean_t], eng="act")
            c.tt(var[:], mean[:], mean[:], ALU.mult, [mean_t], [var_t])
            c.tt(var[:], pe2[:, :], var[:], ALU.subtract, [pe_t, var_t], [var_t])
            c.ts(var[:], var[:], EPS, None, ALU.add, None, [var_t], [var_t])
            c.act(rstd[:], var[:], AF.Sqrt, [var_t], [rstd_t])
            c.op("dve", lambda e: e.reciprocal(out=rstd[:], in_=rstd[:]), [rstd_t], [rstd_t])
            for ch in range(4):
                tn_t, tn = tn_r.next()
                c.tt(tn[:], acc[:, ch, :], mean[:], ALU.subtract, [acc_t, mean_t], [tn_t])
                c.tt(tn[:], tn[:], rstd[:], ALU.mult, [tn_t, rstd_t], [tn_t])
                c.act(ya[:, ch, :], tn[:], AF.Silu, [tn_t, pp_t], [ya_t],
                      bias=pp[:, 132 + ch:133 + ch], scale=pp[:, 128 + ch:129 + ch])
            K1_ = 1.5957691216057308

            def gelu_p1(src_ap, src_t):
                a_t, a_ = gt_r.next()
                b_t, b_ = gt_r.next()
                c.act(a_[:], src_ap, AF.Square, [src_t], [a_t])
                c.ts(a_[:], a_[:], 0.044715, 1.0, ALU.mult, ALU.add, [a_t], [a_t])
                c.tt(b_[:], a_[:], src_ap, ALU.mult, [a_t, src_t], [b_t])
                c.act(a_[:], b_[:], AF.Sigmoid, [b_t], [a_t], scale=K1_)
                return a_t, a_

            uS = {}

            def uA(ch):
                pu_t, pu = c.ps()
                for kc in range(8):
                    c.mm(pu_t, pu[:, :], wuv[:, kc, ch * 128:(ch + 1) * 128], hT[:, kc, :], kc == 0, kc == 7,
                         [wuv_t, hT_t])
                uS[ch] = (pu_t, pu) + gelu_p1(pu[:, :], pu_t)

            def uB(ch):
                pu_t, pu, a_t, a_ = uS.pop(ch)
                c.tt(uT[:, ch, :], a_[:], pu[:, :], ALU.mult, [a_t, pu_t], [uT_t])

            for step in range(4 + 1):
                if step < 4:
                    uA(step)
                if step >= 1:
                    uB(step - 1)

            vA_, vB_, vC_ = {}, {}, {}

            def vA(n):
                pv_t, pv = c.ps()
                for kc in range(8):
                    c.mm(pv_t, pv[:, :], hT[:, kc, n * 128:(n + 1) * 128], wuv[:, kc, 512:1024], kc == 0, kc == 7,
                         [hT_t, wuv_t])
                vA_[n] = (pv_t, pv) + gelu_p1(pv[:, :], pv_t)

            def vB(n):
                pv_t, pv, a_t, a_ = vA_.pop(n)
                vg_t, vg = vg_r.next()
                c.tt(vg[:], a_[:], pv[:, :], ALU.mult, [a_t, pv_t], [vg_t])
                bst_t, bst = bst_r.next()
                c.op("dve", lambda e, o=bst, i=vg: e.bn_stats(out=o[:], in_=i[:]), [vg_t], [bst_t])
                bmv_t, bmv = bmv_r.next()
                c.op("dve", lambda e, o=bmv, i=bst: e.bn_aggr(out=o[:], in_=i[:]), [bst_t], [bmv_t])
                c.ts(bmv[:, 1:2], bmv[:, 1:2], EPS, None, ALU.add, None, [bmv_t], [bmv_t])
                c.act(bmv[:, 1:2], bmv[:, 1:2], AF.Sqrt, [bmv_t], [bmv_t])
                vB_[n] = (vg_t, vg, bmv_t, bmv)

            def vC(n):
                vg_t, vg, bmv_t, bmv = vB_.pop(n)
                c.op("dve", lambda e, r=bmv: e.reciprocal(out=r[:, 1:2], in_=r[:, 1:2]), [bmv_t], [bmv_t])
                c.ts(vg[:], vg[:], bmv[:, 0:1], bmv[:, 1:2], ALU.subtract, ALU.mult, [vg_t, bmv_t], [vg_t])
                c.tt(vg[:], vg[:], lg[:], ALU.mult, [vg_t, lg_t], [vg_t])
                vl_t, vl = vl_r.next()
                c.tt(vl[:], vg[:], lb[:], ALU.add, [vg_t, lb_t], [vl_t])
                px_t, px = c.ps()
                for gi in range(4):
                    c.mm(px_t, px[:, gi * 128:(gi + 1) * 128], vl[:, gi * 128:(gi + 1) * 128], ws[:, gi, :],
                         True, True, [vl_t, ws_t])
                vC_[n] = (px_t, px)

            def vD(n):
                px_t, px = vC_.pop(n)
                tp_t, tp = tp_r.next()
                c.tt(tp[:], px[:, :], bs[:], ALU.add, [px_t, bs_t], [tp_t])
                c.tt(yb[:, :, n * 128:(n + 1) * 128], tp[:].rearrange("p (g t) -> p g t", t=128),
                     uT[:, :, n * 128:(n + 1) * 128], ALU.mult, [tp_t, uT_t], [yb_t])

            for step in range(4 + 3):
                if step < 4:
                    vA(step)
                if 0 <= step - 1 < 4:
                    vB(step - 1)
                if 0 <= step - 2 < 4:
                    vC(step - 2)
                if 0 <= step - 3 < 4:
                    vD(step - 3)
            ys = [(ya_t, ya), (yb_t, yb), (yc_t, ycs)]
            pend = conv_ops(g + 1) if g + 1 < NG else []
            for i in range(3):
                y_t, y = ys[i]
                for hh in range(2):
                    wgt_t, wgt, wb_t, wb = nxtw
                    step = (g * 3 + i) * 2 + hh + 1
                    if step < NG * 6:
                        nxtw = gload((step // 2) % 3, step % 2)
                    for dq in range(4):
                        dc = hh * 4 + dq
                        pz_t, pz = c.ps()
                        ppj_t, ppj = c.ps()
                        for kc in range(8):
                            c.mm(pz_t, pz[:, :], wgt[:, kc, dq * 128:(dq + 1) * 128], hT[:, kc, :], kc == 0, kc == 7,
                                 [wgt_t, hT_t])
                        for kc in range(4):
                            c.mm(ppj_t, ppj[:, :], wb[:, kc, dq * 128:(dq + 1) * 128], y[:, kc, :], kc == 0, kc == 3,
                                 [wb_t, y_t])
                        gs_t, gs = gs_r.next()
                        c.act(gs[:], pz[:, :], AF.Sigmoid, [pz_t, pp_t], [gs_t],
                              bias=pp[:, 136 + i * 8 + dc:137 + i * 8 + dc])
                        if i == 0:
                            c.tt(m[:, dc, :], gs[:], ppj[:, :], ALU.mult, [gs_t, ppj_t], [m_t])
                        else:
                            tp_t, tp = tp_r.next()
                            c.tt(tp[:], gs[:], ppj[:, :], ALU.mult, [gs_t, ppj_t], [tp_t])
                            if i == 1:
                                c.tt(m[:, dc, :], m[:, dc, :], tp[:], ALU.add, [m_t, tp_t], [m_t])
                            else:
                                c.tt(mg[:, dc, :], m[:, dc, :], tp[:], ALU.add, [m_t, tp_t], [mg_t])
                        for _ in range(6):
                            if pend:
                                pend.pop(0)()
            while pend:
                pend.pop(0)()
            for n in range(4):
                for half in range(2):
                    po_t, po = c.ps()
                    for kc in range(8):
                        c.mm(po_t, po[:, :], mg[:, kc, n * 128:(n + 1) * 128],
                             wo[:, kc, half * 512:(half + 1) * 512], kc == 0, kc == 7, [mg_t, wo_t])
                    xs = xg[:, n, half * 512:(half + 1) * 512]
                    c.tt(xs, xs, po[:, :], ALU.add, [xg_t, po_t], [xg_t])
            c.dma(xview(x_dst, g), xg[:], [xg_t], [xt_dst[g]], xk)
        c.end([pfx + "xg0"])
        c.ps_n = 8


def stage_final(c, pfx, x_src, xt_src, norm_d, out_d):
    with contextlib.ExitStack() as st:
        c.begin()
        xg_r = c.rot(st, pfx + "xg", [128, 4, 1024], F32, 2)
        R = norm_rots(c, st, pfx)
        gam_t, gam = c.sb(st, pfx + "gam", [128, 1024], F32)
        c.dma(gam[:], norm_d.to_broadcast([128, 1024]), [], [gam_t], pfx + "gam")
        for g in range(NG):
            xg_t, xg = xg_r.next()
            xk = f"{pfx}xg{(xg_r.i - 1) % 2}"
            c.dma(xg[:], xview(x_src, g), [xt_src[g]], [xg_t], xk)
            for n in range(4):
                st_t, stt_ = R["st"].next()
                c.op("dve", lambda e, o=stt_, x=xg, n=n: e.bn_stats(out=o[:, 0:6], in_=x[:, n, 0:512]), [xg_t], [st_t])
                c.op("dve", lambda e, o=stt_, x=xg, n=n: e.bn_stats(out=o[:, 6:12], in_=x[:, n, 512:1024]), [xg_t],
                     [st_t])
                mv_t, mv = R["mv"].next()
                c.op("dve", lambda e, o=mv, i=stt_: e.bn_aggr(out=o[:, 0:2], in_=i[:, 0:12]), [st_t], [mv_t])
                ms_t, msq = R["ms"].next()
                c.stt(msq[:], mv[:, 0:1], mv[:, 0:1], mv[:, 1:2], ALU.mult, ALU.add, [mv_t], [ms_t])
                c.ts(msq[:], msq[:], EPS, None, ALU.add, None, [ms_t], [ms_t])
                rs_t, rs = R["rs"].next()
                c.act(rs[:], msq[:], AF.Sqrt, [ms_t], [rs_t])
                c.op("dve", lambda e, r=rs: e.reciprocal(out=r[:], in_=r[:]), [rs_t], [rs_t])
                c.stt(xg[:, n, :], xg[:, n, :], rs[:, 0:1], gam[:], ALU.mult, ALU.mult, [xg_t, rs_t, gam_t], [xg_t])
            c.dma(xview(out_d, g), xg[:], [xg_t], [], xk)
        c.end([f"{pfx}xg0", f"{pfx}xg1"])


def build_tok(do_mix, do_p1, do_final, dbg=None):
    nc = bass.Bass("TRN2", target_bir_lowering=False)
    x_in = din(nc, "x_in", [NTOK, D], F32)
    identf_d = din(nc, "identf", [128, 128], F32)
    if do_mix:
        a = dict(
            mix_norm=din(nc, "a_mix_norm", [1, D], F32), w_uv=din(nc, "a_w_uv", [D, 1024], F32),
            w_gate=din(nc, "a_w_gate", [D, 3072], F32), w_branch=din(nc, "a_w_branch", [3, 512, D], F32),
            w_out=din(nc, "a_w_out", [D, D], F32), pp=din(nc, "a_pp", [128, NPP], F32),
            ygh=din(nc, "a_ygh", [512, NTOK + 30], F32), ycT=din(nc, "a_ycT", [512, NTOK], BF16),
            sgu_wT=din(nc, "a_sgu_wT", [4, 128, 128], F32), sgu_b=din(nc, "a_sgu_b", [1, 512], F32),
            sgu_g=din(nc, "a_sgu_g", [1, 512], F32), sgu_bb=din(nc, "a_sgu_bb", [1, 512], F32),
            tril=din(nc, "tril", [128, 128], F32), cdiag=din(nc, "a_cdiag", [4, 128, 31 * 128], F32),
            ffn2_norm=din(nc, "a_ffn2_norm", [1, D], F32), ffn2_wi=din(nc, "a_ffn2_wi", [D, 2 * DFF], F32),
            ffn2_wo=din(nc, "a_ffn2_wo", [DFF, D], F32))
    if do_p1:
        b = dict(
            ffn1_norm=din(nc, "b_ffn1_norm", [1, D], F32), ffn1_wi=din(nc, "b_ffn1_wi", [D, 2 * DFF], F32),
            ffn1_wo=din(nc, "b_ffn1_wo", [DFF, D], F32), mix_norm=din(nc, "b_mix_norm", [1, D], F32),
            w_qkv=din(nc, "b_w_qkv", [D, 1536], F32), w_glu=din(nc, "b_w_glu", [D, 1024], F32),
            cosr=din(nc, "cosr", [128, 1024], F32), sinr=din(nc, "sinr", [128, 1024], F32))
        q_o = dout(nc, "q_o", [NTOK, 512], BF16)
        k_o = dout(nc, "k_o", [NTOK, 512], BF16)
        v_o = dout(nc, "v_o", [NTOK, 512], BF16)
        yglu_o = dout(nc, "yglu_o", [512, NTOK], F32)
    if do_final:
        fin = din(nc, "final_norm", [1, D], F32)
    x_out = dout(nc, "x_out", [NTOK, D], F32)
    if do_mix:
        xs1 = nc.dram_tensor("xs1", [NTOK, D], F32, kind="Internal").ap()
        xs2 = nc.dram_tensor("xs2", [NTOK, D], F32, kind="Internal").ap()
    fresh = lambda nm: [T(f"{nm}{g}") for g in range(NG)]
    with contextlib.ExitStack() as st:
        c = Ctx(nc, st)
        cur = x_in
        if do_mix:
            stage_mixer(c, "m_", cur, xs1, fresh("a"), fresh("b"), a["mix_norm"], a["w_uv"], a["w_gate"],
                        a["w_branch"], a["w_out"], a["pp"], a["ygh"], a["ycT"], a["sgu_wT"], a["sgu_b"],
                        a["sgu_g"], a["sgu_bb"], a["tril"], identf_d, a["cdiag"])
            stage_ffn(c, "f2_", xs1, xs2, fresh("a"), fresh("b"), a["ffn2_norm"], a["ffn2_wi"], a["ffn2_wo"],
                      identf_d)
            cur = xs2
        if do_p1:
            if dbg != "noffn":
                stage_ffn(c, "f1_", cur, x_out, fresh("a"), fresh("b"), b["ffn1_norm"], b["ffn1_wi"], b["ffn1_wo"],
                          identf_d)
            if dbg != "noproj":
                    stage_proj(c, "p_", (x_in if dbg == "noffn" else x_out), fresh("a"), b["mix_norm"], b["w_qkv"], b["w_glu"], b["cosr"], b["sinr"],
                           q_o, k_o, v_o, yglu_o, identf_d)
        if do_final:
            stage_final(c, "fn_", cur, fresh("a"), fin, x_out)
    return nc


_NC = {}


def _get(name, fn):
    if name not in _NC:
        _NC[name] = fn()
    return _NC[name]


def _bf(a):
    return np.ascontiguousarray(a).astype(ml_dtypes.bfloat16) if a.dtype != ml_dtypes.bfloat16 else np.ascontiguousarray(a)


def _consts():
    f = np.float32
    identf = np.eye(128, dtype=f)
    tril = (np.arange(128)[:, None] <= np.arange(128)[None, :]).astype(f)
    pos = np.arange(SEQ, dtype=f)
    inv = (np.float32(500000.0) ** (-(np.arange(0, 16, 2, dtype=f)) / np.float32(16))).astype(f)
    ang = (pos[:, None] * inv[None, :]).astype(f)
    cosr = np.tile(np.cos(ang).astype(f), (1, 8))
    sinr = np.tile(np.sin(ang).astype(f), (1, 8))
    kaug = np.zeros((32, SEQ), dtype=f)
    for j in range(32):
        kaug[j, j * 256:(j + 1) * 256] = 1.0
    qb = np.arange(64) // 2
    pastneg = np.where(np.arange(32)[None, :] < qb[:, None], 0.0, -1e30).astype(f).reshape(1, 2048)
    ownm = (np.arange(32)[None, :] == qb[:, None]).astype(f).reshape(1, 2048)
    cm = np.zeros((4, 128, 512), dtype=f)
    for i in range(4):
        kk = i * 128 + np.arange(128)[:, None]
        cm[i] = np.where(kk <= np.arange(512)[None, :], 0.0, NEG)
    return dict(identf=identf, tril=tril, cosr=cosr, sinr=sinr, kaug=_bf(kaug), pastneg=pastneg, ownm=ownm,
                cmask=_bf(cm))


def _conv_diag(conv_w):
    d = np.zeros((4, 128, 31, 128), dtype=np.float32)
    p = np.arange(128)
    for ch in range(4):
        d[ch, p, :, p] = conv_w[:, ch * 128:(ch + 1) * 128].T
    return d.reshape(4, 128, 31 * 128)


def _pack_pp(conv_w, conv_b, ln_g, ln_b, gate_b):
    pp = np.zeros((128, NPP), dtype=np.float32)
    pp[:, 0:124] = conv_w.reshape(31, 4, 128).transpose(2, 1, 0).reshape(128, 124)
    pp[:, 124:128] = conv_b.reshape(4, 128).T
    pp[:, 128:132] = ln_g.reshape(4, 128).T
    pp[:, 132:136] = ln_b.reshape(4, 128).T
    pp[:, 136:160] = gate_b.reshape(3, 8, 128).transpose(2, 0, 1).reshape(128, 24)
    return pp


def _run(nc, maps):
    res = run_bass_kernel_spmd(nc, maps, core_ids=list(range(8)))
    return res.results


def kernel(x, ffn1_norm, ffn1_wi, ffn1_wo, mix_norm, w_in, conv_w, conv_b, conv_ln_g, conv_ln_b, sgu_ln_g,
           sgu_ln_b, sgu_w, sgu_b, w_branch, gate_b, w_out, ffn2_norm, ffn2_wi, ffn2_wo, final_norm):
    A = lambda v: np.ascontiguousarray(np.asarray(v, dtype=np.float32))
    x = A(x)
    P = dict(ffn1_norm=A(ffn1_norm), ffn1_wi=A(ffn1_wi), ffn1_wo=A(ffn1_wo), mix_norm=A(mix_norm), w_in=A(w_in),
             conv_w=A(conv_w), conv_b=A(conv_b), conv_ln_g=A(conv_ln_g), conv_ln_b=A(conv_ln_b),
             sgu_ln_g=A(sgu_ln_g), sgu_ln_b=A(sgu_ln_b), sgu_w=A(sgu_w), sgu_b=A(sgu_b), w_branch=A(w_branch),
             gate_b=A(gate_b), w_out=A(w_out), ffn2_norm=A(ffn2_norm), ffn2_wi=A(ffn2_wi), ffn2_wo=A(ffn2_wo))
    final_norm = A(final_norm)
    C = _consts()
    xs = [x[c // 4, (c % 4) * NTOK:(c % 4 + 1) * NTOK, :] for c in range(8)]

    def p1_inputs(l):
        w = P["w_in"][l]
        return dict(b_ffn1_norm=P["ffn1_norm"][l][None], b_ffn1_wi=P["ffn1_wi"][l], b_ffn1_wo=P["ffn1_wo"][l],
                    b_mix_norm=P["mix_norm"][l][None], b_w_qkv=A(w[:, 2048:3584]), b_w_glu=A(w[:, 0:1024]))

    def mix_inputs(l):
        w = P["w_in"][l]
        return dict(a_mix_norm=P["mix_norm"][l][None], a_w_uv=A(w[:, 1024:2048]), a_w_gate=A(w[:, 3584:6656]),
                    a_w_branch=P["w_branch"][l], a_w_out=P["w_out"][l],
                    a_pp=_pack_pp(P["conv_w"][l], P["conv_b"][l], P["conv_ln_g"][l], P["conv_ln_b"][l],
                                  P["gate_b"][l]),
                    a_cdiag=_conv_diag(P["conv_w"][l]),
                    a_sgu_wT=A(P["sgu_w"][l].transpose(0, 2, 1)), a_sgu_b=P["sgu_b"][l].reshape(1, 512),
                    a_sgu_g=P["sgu_ln_g"][l][None], a_sgu_bb=P["sgu_ln_b"][l][None], tril=C["tril"],
                    a_ffn2_norm=P["ffn2_norm"][l][None], a_ffn2_wi=P["ffn2_wi"][l], a_ffn2_wo=P["ffn2_wo"][l])

    def rope_tabs(c):
        o = (c % 4) * NTOK
        pm = lambda t: A(t[o:o + NTOK].reshape(16, 128, 64).transpose(1, 0, 2).reshape(128, 1024))
        return dict(cosr=pm(C["cosr"]), sinr=pm(C["sinr"]))

    def attention(res):
        q = np.stack([np.concatenate([res[b * 4 + j]["q_o"] for j in range(4)], 0) for b in range(2)])
        k = np.stack([np.concatenate([res[b * 4 + j]["k_o"] for j in range(4)], 0) for b in range(2)])
        v = np.stack([np.concatenate([res[b * 4 + j]["v_o"] for j in range(4)], 0) for b in range(2)])
        maps = []
        for h in range(8):
            sl = slice(h * 64, (h + 1) * 64)
            maps.append(dict(qT=_bf(q[:, :, sl].transpose(0, 2, 1)), kT=_bf(k[:, :, sl].transpose(0, 2, 1)),
                             v=_bf(v[:, :, sl]), kaug=C["kaug"], pastneg=C["pastneg"], ownm=C["ownm"],
                             cmask=C["cmask"], identf=C["identf"]))
        r = _run(_get("att", build_att), maps)
        Y = np.concatenate([np.asarray(r[h]["yc"]) for h in range(8)], axis=2)
        G = [np.concatenate([np.asarray(res[b * 4 + j]["yglu_o"]) for j in range(4)], 1) for b in range(2)]
        ycT, ygh = [], []
        for c in range(8):
            b, o = c // 4, (c % 4) * NTOK
            ycT.append(_bf(Y[b, o:o + NTOK, :].T))
            gh = np.zeros((512, NTOK + 30), dtype=np.float32)
            gh[:, 30:] = G[b][:, o:o + NTOK]
            if o > 0:
                gh[:, :30] = G[b][:, o - 30:o]
            ygh.append(gh)
        return ycT, ygh

    maps = [dict(x_in=xs[c], identf=C["identf"], **p1_inputs(0), **rope_tabs(c)) for c in range(8)]
    res = _run(_get("k1", lambda: build_tok(False, True, False)), maps)
    xcur = [np.asarray(res[c]["x_out"]) for c in range(8)]
    ycT, ygh = attention(res)
    maps = [dict(x_in=xcur[c], identf=C["identf"], a_ygh=ygh[c], a_ycT=ycT[c], **mix_inputs(0), **p1_inputs(1),
                 **rope_tabs(c)) for c in range(8)]
    res = _run(_get("k2", lambda: build_tok(True, True, False)), maps)
    xcur = [np.asarray(res[c]["x_out"]) for c in range(8)]
    ycT, ygh = attention(res)
    maps = [dict(x_in=xcur[c], identf=C["identf"], a_ygh=ygh[c], a_ycT=ycT[c], final_norm=final_norm[None],
                 **mix_inputs(1)) for c in range(8)]
    res = _run(_get("k3", lambda: build_tok(True, False, True)), maps)
    out = np.stack([np.concatenate([np.asarray(res[b * 4 + j]["x_out"]) for j in range(4)], 0) for b in range(2)])
    return out.astype(np.float32)
```

```python
import contextlib
import numpy as np
import ml_dtypes
import concourse.bass as bass
import concourse.mybir as mybir
from concourse.bass_utils import run_bass_kernel_spmd

F32, BF16 = mybir.dt.float32, mybir.dt.bfloat16
AF = mybir.ActivationFunctionType
ALU = mybir.AluOpType
AX = mybir.AxisListType

D = 1024
DFF = 2816
SEQ = 8192
NTOK = 2048
TG = 512
NG = NTOK // TG
EPS = 1e-6
NPP = 160
NEG = -30000.0
GELU_TANH_NATIVE = False


class T:
    __slots__ = ("name", "w", "r", "psum")

    def __init__(self, name, psum=False):
        self.name, self.w, self.r, self.psum = name, None, {}, psum


class Sched:
    ENG = ("pe", "act", "dve", "pool", "sp")

    def __init__(self, nc):
        self.nc = nc
        self.ops = []
        self.keytot = {}

    def op(self, eng, fn, reads=(), writes=(), key=None):
        idx = len(self.ops)
        is_dma = key is not None
        deps = set()

        def add(d, kind):
            o = self.ops[d]
            if o["key"] is not None:
                deps.add(("k", o["key"], self.keytot[o["key"]]))
            else:
                if o["eng"] == eng and not is_dma:
                    if eng == "pe" or kind == "war":
                        return
                deps.add(("e", d))

        for t in reads:
            if t.w is not None:
                add(t.w, "raw")
            if t.psum:
                for rk_, d in t.r.items():
                    if rk_ != eng:
                        add(d, "raw")
        for t in writes:
            if t.w is not None:
                add(t.w, "waw")
            for d in t.r.values():
                add(d, "war")
        if is_dma:
            self.keytot[key] = self.keytot.get(key, 0) + 16
        self.ops.append(dict(eng=eng, fn=fn, deps=deps, key=key))
        rk = ("k", key) if is_dma else eng
        for t in reads:
            t.r[rk] = idx
        for t in writes:
            t.w = idx
            t.r = {}
        return idx

    def emit(self, stack, final_keys, tag=""):
        nc = self.nc
        need = set(d[1] for o in self.ops for d in o["deps"] if d[0] == "e")
        cnt = {e: 0 for e in self.ENG}
        ms = {}
        for i, o in enumerate(self.ops):
            if i in need:
                cnt[o["eng"]] += 1
                ms[i] = cnt[o["eng"]]
        assert max(cnt.values()) < 60000, cnt
        sem_e = {e: stack.enter_context(nc.semaphore(f"se_{tag}{e}")) for e in self.ENG}
        sem_k = {k: stack.enter_context(nc.semaphore(f"sk_{tag}{k}")) for k in self.keytot}
        per = {e: [] for e in self.ENG}
        for i, o in enumerate(self.ops):
            per[o["eng"]].append(i)
        ops = self.ops
        keytot = self.keytot

        def run(e, eo):
            seen_e = {f: 0 for f in self.ENG}
            seen_k = {}
            for i in per[e]:
                o = ops[i]
                for d in sorted(o["deps"], key=str):
                    if d[0] == "e":
                        f = ops[d[1]]["eng"]
                        v = ms[d[1]]
                        if seen_e[f] < v:
                            eo.wait_ge(sem_e[f], v)
                            seen_e[f] = v
                    else:
                        _, k, v = d
                        if seen_k.get(k, 0) < v:
                            eo.wait_ge(sem_k[k], v)
                            seen_k[k] = v
                ins = o["fn"](eo)
                if o["key"] is not None:
                    ins.then_inc(sem_k[o["key"]], 16)
                elif i in ms:
                    ins.then_inc(sem_e[e], 1)
            if e == "sp":
                for k in final_keys:
                    eo.wait_ge(sem_k[k], keytot[k])

        with nc.Block() as block:
            @block.tensor
            def _(eo):
                run("pe", eo)

            @block.scalar
            def _(eo):
                run("act", eo)

            @block.vector
            def _(eo):
                run("dve", eo)

            @block.gpsimd
            def _(eo):
                run("pool", eo)

            @block.sync
            def _(eo):
                run("sp", eo)


class Ctx:
    def __init__(self, nc, stack):
        self.nc, self.stack = nc, stack
        self.s = None
        self.psum = []
        for i in range(8):
            t = stack.enter_context(nc.psum_tensor(f"ps{i}", [128, 512], F32))
            self.psum.append((T(f"ps{i}", psum=True), t))
        self.pi = 0
        self.nstage = 0

    def begin(self):
        self.s = Sched(self.nc)
        for t, _ in self.psum:
            t.w, t.r = None, {}

    def end(self, final_keys):
        self.s.emit(self.stack, final_keys, tag=f"s{self.nstage}_")
        self.nstage += 1

    def ps(self):
        r = self.psum[self.pi % getattr(self, "ps_n", 8)]
        self.pi += 1
        return r

    def sb(self, stack, name, shape, dt):
        t = stack.enter_context(self.nc.sbuf_tensor(name, shape, dt))
        return T(name), t

    def rot(self, stack, name, shape, dt, n):
        return Rot([self.sb(stack, f"{name}{i}", shape, dt) for i in range(n)])

    def op(self, *a, **k):
        return self.s.op(*a, **k)

    def dma(self, out, in_, reads, writes, key, q="sp"):
        self.s.op(q, lambda e: e.dma_start(out=out, in_=in_), reads, writes, key=key)

    def mm(self, pt, out, lhsT, rhs, start, stop, reads):
        self.s.op("pe", lambda e: e.matmul(out=out, lhsT=lhsT, rhs=rhs, start=start, stop=stop),
                  reads, [pt])

    def tr(self, pt, out, in_, ident, reads):
        self.s.op("pe", lambda e: e.transpose(out=out, in_=in_, identity=ident), reads, [pt])

    def act(self, out, in_, func, reads, writes, bias=None, scale=None):
        kw = {}
        if bias is not None:
            kw["bias"] = bias
        if scale is not None:
            kw["scale"] = scale
        self.s.op("act", lambda e: e.activation(out=out, in_=in_, func=func, **kw), reads, writes)

    def tt(self, out, in0, in1, op, reads, writes, eng="dve"):
        self.s.op(eng, lambda e: e.tensor_tensor(out=out, in0=in0, in1=in1, op=op), reads, writes)

    def ts(self, out, in0, s1, s2, op0, op1, reads, writes, eng="dve"):
        if op1 is None:
            self.s.op(eng, lambda e: e.tensor_scalar(out=out, in0=in0, scalar1=s1, scalar2=None, op0=op0),
                      reads, writes)
        else:
            self.s.op(eng, lambda e: e.tensor_scalar(out=out, in0=in0, scalar1=s1, scalar2=s2, op0=op0, op1=op1),
                      reads, writes)

    def stt(self, out, in0, scalar, in1, op0, op1, reads, writes):
        self.s.op("dve", lambda e: e.scalar_tensor_tensor(out=out, in0=in0, scalar=scalar, in1=in1,
                                                          op0=op0, op1=op1), reads, writes)

    def cp(self, out, in_, reads, writes, eng="dve"):
        if eng == "act":
            self.s.op("act", lambda e: e.copy(out=out, in_=in_), reads, writes)
        else:
            self.s.op(eng, lambda e: e.tensor_copy(out=out, in_=in_), reads, writes)

    def memset(self, out, val, writes, eng="dve"):
        self.s.op(eng, lambda e: e.memset(out, val), [], writes)


class Rot:
    def __init__(self, items):
        self.items, self.i = items, 0

    def next(self):
        r = self.items[self.i % len(self.items)]
        self.i += 1
        return r


def din(nc, name, shape, dt):
    return nc.dram_tensor(name, list(shape), dt, kind="ExternalInput").ap()


def dout(nc, name, shape, dt):
    return nc.dram_tensor(name, list(shape), dt, kind="ExternalOutput").ap()


def build_att():
    nc = bass.Bass("TRN2", target_bir_lowering=False)
    qT = din(nc, "qT", [2, 64, SEQ], BF16)
    kT = din(nc, "kT", [2, 64, SEQ], BF16)
    vv = din(nc, "v", [2, SEQ, 64], BF16)
    kaug = din(nc, "kaug", [32, SEQ], BF16)
    pastneg = din(nc, "pastneg", [1, 2048], F32)
    ownm = din(nc, "ownm", [1, 2048], F32)
    cmask = din(nc, "cmask", [4, 128, 512], BF16)
    identf_d = din(nc, "identf", [128, 128], F32)
    yc = dout(nc, "yc", [2, SEQ, 64], BF16)
    with contextlib.ExitStack() as st:
        c = Ctx(nc, st)
        c.begin()
        QA = [c.sb(st, f"QA{i}", [96, SEQ], BF16) for i in range(2)]
        QB = [T(f"QAb{i}") for i in range(2)]
        KA = [c.sb(st, f"KA{i}", [96, SEQ], BF16) for i in range(2)]
        VA = [c.sb(st, f"VA{i}", [128, 64, 65], BF16) for i in range(2)]
        pn_t, pn = c.sb(st, "pastneg_sb", [128, 64, 32], F32)
        ow_t, ow = c.sb(st, "own_sb", [128, 64, 32], F32)
        cm_t, cm = c.sb(st, "cm_sb", [128, 4, 512], BF16)
        idf_t, idf = c.sb(st, "identf_sb", [128, 128], F32)
        idb_t, idb = c.sb(st, "identb_sb", [128, 128], BF16)
        kmf_t, kmf = c.sb(st, "kmf", [64, 32], F32)
        qf_t, qf = c.sb(st, "qf", [64, SEQ], F32)
        gm_r = c.rot(st, "gm", [128, 4, 32], F32, 3)
        m8_r = c.rot(st, "m8", [128, 4, 8], F32, 3)
        thr_r = c.rot(st, "thr", [128, 4], F32, 3)
        b2_r = c.rot(st, "b2", [128, 4, 32], F32, 3)
        b96_r = c.rot(st, "b96", [128, 4, 96], F32, 3)
        pT_r = c.rot(st, "pT", [128, 512], BF16, 4)
        rc_r = c.rot(st, "rc", [128, 1], F32, 4)
        yo_r = c.rot(st, "yo", [128, 4, 64], BF16, 2)

        c.dma(pn[:].rearrange("p a b -> p (a b)"), pastneg.to_broadcast([128, 2048]), [], [pn_t], "pn")
        c.dma(ow[:].rearrange("p a b -> p (a b)"), ownm.to_broadcast([128, 2048]), [], [ow_t], "ow")
        c.dma(cm[:], cmask.rearrange("i p q -> p i q"), [], [cm_t], "cm")
        c.dma(idf[:], identf_d, [], [idf_t], "idf")
        c.cp(idb[:], idf[:], [idf_t], [idb_t])
        for (bt, b96) in b96_r.items:
            c.memset(b96[:], 0.0, [bt])
        for i in range(2):
            c.dma(KA[i][1][64:96, :], kaug, [], [KA[i][0]], f"KA{i}")
            c.memset(VA[i][1][:, :, 64:65], 1.0, [VA[i][0]])

        def load(b):
            qa_t, qa = QA[b % 2]
            ka_t, ka = KA[b % 2]
            va_t, va = VA[b % 2]
            c.dma(qa[0:64, :], qT[b], [], [qa_t], f"QA{b % 2}")
            c.dma(ka[0:64, :], kT[b], [], [ka_t], f"KA{b % 2}")
            vsrc = vv[b].rearrange("(n p) d -> p n d", p=128)
            for j in range(8):
                c.dma(va[:, j * 8:(j + 1) * 8, 0:64], vsrc[:, j * 8:(j + 1) * 8, :], [], [va_t], f"VA{b % 2}")

        load(0)
        for b in range(2):
            qa_t, qa = QA[b % 2]
            qb_t = QB[b % 2]
            ka_t, ka = KA[b % 2]
            va_t, va = VA[b % 2]
            if b + 1 < 2:
                load(b + 1)
            for hq in range(4):
                c.cp(qf[:, hq * 2048:(hq + 1) * 2048], qa[0:64, hq * 2048:(hq + 1) * 2048], [qa_t], [qf_t],
                     eng=("pool" if hq % 2 == 0 else "dve"))
            c.op("dve", lambda e, ka=ka: e.tensor_reduce(
                out=kmf[:], in_=ka[0:64, :].rearrange("p (j k) -> p j k", k=256), axis=AX.X, op=ALU.add),
                [ka_t], [kmf_t])
            c.ts(kmf[:], kmf[:], 1.0 / 256.0, None, ALU.mult, None, [kmf_t], [kmf_t])
            stA, stB = {}, {}

            def gA(G_):
                pg_t, pg = c.ps()
                for j in range(4):
                    qt = 4 * G_ + j
                    c.mm(pg_t, pg[:, j * 32:(j + 1) * 32], qf[:, qt * 128:(qt + 1) * 128], kmf[:], True, True,
                         [qf_t, kmf_t])
                stA[G_] = (pg_t, pg)

            def gB(G_):
                pg_t, pg = stA.pop(G_)
                gm_t, gm = gm_r.next()
                c.tt(gm[:], pg[:, 0:128].rearrange("p (j n) -> p j n", n=32), pn[:, 4 * G_:4 * G_ + 4, :], ALU.add,
                     [pg_t, pn_t], [gm_t])
                m8_t, m8 = m8_r.next()
                for j in range(4):
                    c.op("dve", lambda e, m8=m8, gm=gm, j=j: e.max(out=m8[:, j, :], in_=gm[:, j, :]), [gm_t], [m8_t])
                th_t, th = thr_r.next()
                c.ts(th[:], m8[:, :, 2], -1e29, None, ALU.max, None, [m8_t], [th_t])
                b2_t, b2 = b2_r.next()
                for j in range(4):
                    c.stt(b2[:, j, :], gm[:, j, :], th[:, j:j + 1], ow[:, 4 * G_ + j, :], ALU.is_ge, ALU.add,
                          [gm_t, th_t, ow_t], [b2_t])
                b96_t, b96 = b96_r.next()
                c.ts(b96[:, :, 64:96], b2[:], -NEG, NEG, ALU.mult, ALU.add, [b2_t], [b96_t])
                stB[G_] = (b96_t, b96)

            def gC(G_):
                b96_t, b96 = stB.pop(G_)
                pt_t, pt = c.ps()
                for j in range(4):
                    c.tr(pt_t, pt[0:96, j * 128:(j + 1) * 128], b96[:, j, :], idf[:], [b96_t, idf_t])
                c.cp(qa[64:96, G_ * 512:(G_ + 1) * 512], pt[64:96, :], [pt_t], [qb_t], eng="act")

            for step in range(16 + 2):
                if step < 16:
                    gA(step)
                if 0 <= step - 1 < 16:
                    gB(step - 1)
                if 0 <= step - 2 < 16:
                    gC(step - 2)
            units = [(G, kt) for G in range(16) for kt in range(4 * G + 4)]
            po = [c.psum[4 + i] for i in range(4)]
            sq = {}

            def qk(u, ka=ka, qa=qa, ka_t=ka_t, qa_t=qa_t, qb_t=qb_t):
                G, kt = units[u]
                s_t, s_ = c.psum[u % 4]
                own = kt >= 4 * G
                c.mm(s_t, s_[:, :], ka[0:96, kt * 128:(kt + 1) * 128], qa[0:96, G * 512:(G + 1) * 512],
                     True, not own, [ka_t, qa_t, qb_t])
                if own:
                    c.mm(s_t, s_[:, :], idb[:], cm[:, kt - 4 * G, :], False, True, [idb_t, cm_t])
                sq[u] = (s_t, s_)

            LA = 3
            for u0 in range(LA):
                qk(u0)
            for u in range(len(units)):
                G, kt = units[u]
                nkt = 4 * G + 4
                if u + LA < len(units):
                    qk(u + LA)
                s_t, s_ = sq.pop(u)
                p_t, p = pT_r.next()
                c.act(p[:], s_[:, :], AF.Exp, [s_t], [p_t], scale=0.125)
                for qi in range(4):
                    c.mm(po[qi][0], po[qi][1][:, 0:65], p[:, qi * 128:(qi + 1) * 128], va[:, kt, :],
                         kt == 0, kt == nkt - 1, [p_t, va_t])
                if kt == nkt - 1:
                    yo_t, yo = yo_r.next()
                    for qi in range(4):
                        r_t, r = rc_r.next()
                        c.op("dve", lambda e, r=r, pp=po[qi][1]: e.reciprocal(out=r[:], in_=pp[:, 64:65]),
                             [po[qi][0]], [r_t])
                        c.ts(yo[:, qi, :], po[qi][1][:, 0:64], r[:, 0:1], None, ALU.mult, None,
                             [po[qi][0], r_t], [yo_t])
                    c.dma(yc[b, G * 512:(G + 1) * 512, :].rearrange("(q p) d -> p q d", p=128), yo[:],
                          [yo_t], [], f"yo{(yo_r.i - 1) % 2}")
        c.end(["yo0", "yo1"])
    return nc


def rmsnorm_group(c, R, xg_t, xg, gam_t, gam, hT_t, hT, ntiles=4):
    idf_t, idf = R["idf"]
    stA, stB = {}, {}

    def A(n):
        st_t, stt_ = R["st"].next()
        c.op("dve", lambda e, o=stt_, x=xg, n=n: e.bn_stats(out=o[:, 0:6], in_=x[:, n, 0:512]), [xg_t], [st_t])
        c.op("dve", lambda e, o=stt_, x=xg, n=n: e.bn_stats(out=o[:, 6:12], in_=x[:, n, 512:1024]), [xg_t], [st_t])
        mv_t, mv = R["mv"].next()
        c.op("dve", lambda e, o=mv, i=stt_: e.bn_aggr(out=o[:, 0:2], in_=i[:, 0:12]), [st_t], [mv_t])
        ms_t, msq = R["ms"].next()
        c.stt(msq[:], mv[:, 0:1], mv[:, 0:1], mv[:, 1:2], ALU.mult, ALU.add, [mv_t], [ms_t])
        c.ts(msq[:], msq[:], EPS, None, ALU.add, None, [ms_t], [ms_t])
        rs_t, rs = R["rs"].next()
        c.act(rs[:], msq[:], AF.Sqrt, [ms_t], [rs_t])
        stA[n] = (rs_t, rs)

    def B(n):
        rs_t, rs = stA.pop(n)
        c.op("dve", lambda e, r=rs: e.reciprocal(out=r[:], in_=r[:]), [rs_t], [rs_t])
        hn_t, hn = R["hn"].next()
        c.stt(hn[:], xg[:, n, :], rs[:, 0:1], gam[:], ALU.mult, ALU.mult, [xg_t, rs_t, gam_t], [hn_t])
        banks = []
        for half in range(2):
            b_t, bk = c.ps()
            for j in range(4):
                kc = half * 4 + j
                c.tr(b_t, bk[:, j * 128:(j + 1) * 128], hn[:, kc * 128:(kc + 1) * 128], idf[:], [hn_t, idf_t])
            banks.append((b_t, bk))
        stB[n] = banks

    def C(n):
        for half, (b_t, bk) in enumerate(stB.pop(n)):
            c.cp(hT[:, half * 4:(half + 1) * 4, n * 128:(n + 1) * 128],
                 bk[:, :].rearrange("p (j t) -> p j t", t=128), [b_t], [hT_t],
                 eng=("act" if half == 0 else "dve"))

    for step in range(ntiles + 2):
        if step < ntiles:
            A(step)
        if 0 <= step - 1 < ntiles:
            B(step - 1)
        if 0 <= step - 2 < ntiles:
            C(step - 2)


def norm_rots(c, st, pfx, identf_d=None, hn_n=2):
    idf = None
    if identf_d is not None:
        idf = c.sb(st, pfx + "idf", [128, 128], F32)
        c.dma(idf[1][:], identf_d, [], [idf[0]], pfx + "idf")
    return dict(
        idf=idf,
        st=c.rot(st, pfx + "st", [128, 12], F32, 4),
        mv=c.rot(st, pfx + "mv", [128, 2], F32, 4),
        ms=c.rot(st, pfx + "ms", [128, 1], F32, 4),
        rs=c.rot(st, pfx + "rs", [128, 1], F32, 4),
        hn=c.rot(st, pfx + "hn", [128, 1024], F32, hn_n),
    )


def xview(x, g):
    return x[g * TG:(g + 1) * TG, :].rearrange("(n p) d -> p n d", p=128)


FG = 1024
NFG = NTOK // FG
FT = FG // 128


def stage_ffn(c, pfx, x_src, x_dst, xt_src, xt_dst, norm_d, wi, wo, identf_d):
    with contextlib.ExitStack() as st:
        c.begin()
        R = norm_rots(c, st, pfx, identf_d, hn_n=2)
        xg_r = c.rot(st, pfx + "xg", [128, FT, 1024], F32, 2)
        hT_t, hT = c.sb(st, pfx + "hT", [128, 8, FG], BF16)
        a_t, act = c.sb(st, pfx + "act", [128, 22, FG], BF16)
        wo_t, wo_sb = c.sb(st, pfx + "wo", [128, 22, 1024], BF16)
        wg_r = c.rot(st, pfx + "wg", [128, 8, 256], BF16, 2)
        wu_r = c.rot(st, pfx + "wu", [128, 8, 256], BF16, 2)
        sg_r = c.rot(st, pfx + "sg", [128, 512], F32, 2)
        gam_t, gam = c.sb(st, pfx + "gam", [128, 1024], F32)
        c.dma(gam[:], norm_d.to_broadcast([128, 1024]), [], [gam_t], pfx + "gam")

        def wload(j):
            g_t, g = wg_r.next()
            u_t, u = wu_r.next()
            k = (wg_r.i - 1) % 2
            c.dma(g[:], wi[:, j * 256:(j + 1) * 256].rearrange("(kc p) n -> p kc n", p=128), [], [g_t],
                  f"{pfx}wg{k}", q="pool")
            c.dma(u[:], wi[:, DFF + j * 256:DFF + (j + 1) * 256].rearrange("(kc p) n -> p kc n", p=128), [], [u_t],
                  f"{pfx}wu{k}", q="pool")
            return (g_t, g, u_t, u)

        nxt = wload(0)
        wo_pend = list(range(22))
        def xload(g):
            t_, b_ = xg_r.next()
            k_ = f"{pfx}xg{(xg_r.i - 1) % 2}"
            c.dma(b_[:], x_src[g * FG:(g + 1) * FG, :].rearrange("(n p) d -> p n d", p=128), [], [t_], k_)
            return t_, b_, k_

        nx = xload(0)
        for g in range(NFG):
            xg_t, xg, xk = nx
            if g + 1 < NFG:
                nx = xload(g + 1)
            rmsnorm_group(c, R, xg_t, xg, gam_t, gam, hT_t, hT, ntiles=FT)
            for j in range(11):
                g_t, gw, u_t, uw = nxt
                if not (g == NFG - 1 and j == 10):
                    nxt = wload((j + 1) % 11)
                for _ in range(2):
                    if wo_pend:
                        kc_ = wo_pend.pop(0)
                        c.dma(wo_sb[:, kc_, :], wo[kc_ * 128:(kc_ + 1) * 128, :], [], [wo_t], pfx + "wo", q="pool")
                for cc in range(2):
                    for hf in range(FG // 512):
                        ts_ = slice(hf * 512, (hf + 1) * 512)
                        pg_t, pg = c.ps()
                        pu_t, pu = c.ps()
                        for kc in range(8):
                            c.mm(pg_t, pg[:, :], gw[:, kc, cc * 128:(cc + 1) * 128], hT[:, kc, ts_], kc == 0, kc == 7,
                                 [g_t, hT_t])
                        for kc in range(8):
                            c.mm(pu_t, pu[:, :], uw[:, kc, cc * 128:(cc + 1) * 128], hT[:, kc, ts_], kc == 0, kc == 7,
                                 [u_t, hT_t])
                        sg_t, sg = sg_r.next()
                        c.act(sg[:], pg[:, :], AF.Silu, [pg_t], [sg_t])
                        c.tt(act[:, 2 * j + cc, ts_], sg[:], pu[:, :], ALU.mult, [sg_t, pu_t], [a_t])
            for n in range(FT):
                for half in range(2):
                    po_t, po = c.ps()
                    for kc in range(22):
                        c.mm(po_t, po[:, :], act[:, kc, n * 128:(n + 1) * 128],
                             wo_sb[:, kc, half * 512:(half + 1) * 512], kc == 0, kc == 21, [a_t, wo_t])
                    xs = xg[:, n, half * 512:(half + 1) * 512]
                    c.stt(xs, po[:, :], 0.5, xs, ALU.mult, ALU.add, [po_t, xg_t], [xg_t])
            c.dma(x_dst[g * FG:(g + 1) * FG, :].rearrange("(n p) d -> p n d", p=128), xg[:], [xg_t], [], xk)
        c.end([f"{pfx}xg{i}" for i in range(min(2, NFG))])


def stage_proj(c, pfx, x_src, xt_src, norm_d, w_qkv, w_glu, cosr, sinr, q_o, k_o, v_o, yglu_o, identf_d):
    with contextlib.ExitStack() as st:
        c.begin()
        R = norm_rots(c, st, pfx, identf_d, hn_n=4)
        xg_r = c.rot(st, pfx + "xg", [128, 4, 1024], F32, 2)
        hT_t, hT = c.sb(st, pfx + "hT", [128, 8, TG], BF16)
        wq_t, wq = c.sb(st, pfx + "wqkv", [128, 8, 1536], BF16)
        wc_t, wc = c.sb(st, pfx + "wglu", [128, 8, 1024], BF16)
        gam_t, gam = c.sb(st, pfx + "gam", [128, 1024], F32)
        cs_t, cs = c.sb(st, pfx + "cos", [128, 16, 64], F32)
        sn_t, sn = c.sb(st, pfx + "sin", [128, 16, 64], F32)
        qo_r = c.rot(st, pfx + "qo", [128, 3, 512], BF16, 2)
        t_r = c.rot(st, pfx + "rt", [128, 4, 64], F32, 2)
        sg_r = c.rot(st, pfx + "sg", [128, TG], F32, 2)
        yg_r = c.rot(st, pfx + "yg", [128, 4, TG], F32, 2)
        c.dma(gam[:], norm_d.to_broadcast([128, 1024]), [], [gam_t], pfx + "gam")
        c.dma(cs[:].rearrange("p n d -> p (n d)"), cosr, [], [cs_t], pfx + "cs")
        c.dma(sn[:].rearrange("p n d -> p (n d)"), sinr, [], [sn_t], pfx + "sn")
        for j in range(3):
            c.dma(wq[:, :, j * 512:(j + 1) * 512],
                  w_qkv[:, j * 512:(j + 1) * 512].rearrange("(kc p) n -> p kc n", p=128), [], [wq_t],
                  pfx + "wq", q="pool")
        for j in range(2):
            c.dma(wc[:, :, j * 512:(j + 1) * 512],
                  w_glu[:, j * 512:(j + 1) * 512].rearrange("(kc p) n -> p kc n", p=128), [], [wc_t],
                  pfx + "wc", q="pool")
        outs = [q_o, k_o, v_o]
        for g in range(NG):
            xg_t, xg = xg_r.next()
            xk = f"{pfx}xg{(xg_r.i - 1) % 2}"
            c.dma(xg[:], xview(x_src, g), [xt_src[g]], [xg_t], xk)
            rmsnorm_group(c, R, xg_t, xg, gam_t, gam, hT_t, hT)
            for n in range(4):
                tile = g * 4 + n
                qo_t, qo = qo_r.next()
                qk = f"{pfx}qo{(qo_r.i - 1) % 2}"
                for w in range(3):
                    p_t, p = c.ps()
                    for kc in range(8):
                        c.mm(p_t, p[:, :], hT[:, kc, n * 128:(n + 1) * 128], wq[:, kc, w * 512:(w + 1) * 512],
                             kc == 0, kc == 7, [hT_t, wq_t])
                    c.cp(qo[:, w, :], p[:, :], [p_t], [qo_t], eng="act")
                    if w < 2:
                        pv = p[:, :].rearrange("p (h d) -> p h d", d=64)
                        ov = qo[:, w, :].rearrange("p (h d) -> p h d", d=64)
                        x1, x2 = pv[:, :, 0:8], pv[:, :, 8:16]
                        cc_ = cs[:, tile, :].rearrange("p (h d) -> p h d", d=8)
                        ss_ = sn[:, tile, :].rearrange("p (h d) -> p h d", d=8)
                        t_t, t = t_r.next()
                        tv = [t[:, i, :].rearrange("p (h d) -> p h d", d=8) for i in range(4)]
                        c.tt(tv[0], x1, cc_, ALU.mult, [p_t, cs_t], [t_t])
                        c.tt(tv[1], x2, ss_, ALU.mult, [p_t, sn_t], [t_t])
                        c.tt(tv[2], x2, cc_, ALU.mult, [p_t, cs_t], [t_t])
                        c.tt(tv[3], x1, ss_, ALU.mult, [p_t, sn_t], [t_t])
                        c.tt(ov[:, :, 0:8], tv[0], tv[1], ALU.subtract, [t_t], [qo_t])
                        c.tt(ov[:, :, 8:16], tv[2], tv[3], ALU.add, [t_t], [qo_t])
                for w in range(3):
                    c.dma(outs[w][tile * 128:(tile + 1) * 128, :], qo[:, w, :], [qo_t], [], qk)
            yg_t, yg = yg_r.next()
            yk = f"{pfx}yg{(yg_r.i - 1) % 2}"
            for ch in range(4):
                pa_t, pa = c.ps()
                pg_t, pg = c.ps()
                for kc in range(8):
                    c.mm(pa_t, pa[:, :], wc[:, kc, ch * 128:(ch + 1) * 128], hT[:, kc, :], kc == 0, kc == 7,
                         [wc_t, hT_t])
                for kc in range(8):
                    c.mm(pg_t, pg[:, :], wc[:, kc, 512 + ch * 128:512 + (ch + 1) * 128], hT[:, kc, :], kc == 0,
                         kc == 7, [wc_t, hT_t])
                sg_t, sg = sg_r.next()
                c.act(sg[:], pg[:, :], AF.Sigmoid, [pg_t], [sg_t])
                c.tt(yg[:, ch, :], sg[:], pa[:, :], ALU.mult, [sg_t, pa_t], [yg_t])
            c.dma(yglu_o.rearrange("(c p) t -> p c t", p=128)[:, :, g * TG:(g + 1) * TG], yg[:], [yg_t], [], yk)
        c.end([f"{pfx}qo0", f"{pfx}qo1", f"{pfx}yg0", f"{pfx}yg1"])


def gelu_tanh(c, out, src_ap, src_t, out_t, tmp_r):
    if GELU_TANH_NATIVE:
        c.act(out, src_ap, AF.Gelu_apprx_tanh, [src_t], [out_t])
        return
    a_t, a = tmp_r.next()
    b_t, b = tmp_r.next()
    c.act(a[:], src_ap, AF.Square, [src_t], [a_t])
    c.ts(a[:], a[:], 0.044715, 1.0, ALU.mult, ALU.add, [a_t], [a_t])
    c.tt(b[:], a[:], src_ap, ALU.mult, [a_t, src_t], [b_t])
    c.act(a[:], b[:], AF.Sigmoid, [b_t], [a_t], scale=1.5957691216057308)
    c.tt(out, a[:], src_ap, ALU.mult, [a_t, src_t], [out_t])


def stage_mixer(c, pfx, x_src, x_dst, xt_src, xt_dst, norm_d, w_uv, w_gate, w_branch, w_out, ppd, ygh_d, ycT_d,
                sgu_wT_d, sgu_b_d, sgu_g_d, sgu_bb_d, tril_d, identf_d, cdiag_d):
    with contextlib.ExitStack() as st:
        c.begin()
        c.ps_n = 7
        R = norm_rots(c, st, pfx, identf_d)
        xg_r = c.rot(st, pfx + "xg", [128, 4, 1024], F32, 1)
        hT_t, hT = c.sb(st, pfx + "hT", [128, 8, TG], BF16)
        gam_t, gam = c.sb(st, pfx + "gam", [128, 1024], F32)
        pp_t, pp = c.sb(st, pfx + "pp", [128, NPP], F32)
        ygh_t, ygh = c.sb(st, pfx + "ygh", [128, 4, TG + 30], BF16)
        cdg_t, cdg = c.sb(st, pfx + "cdg", [128, 31, 128], BF16)
        acc_t, acc = c.sb(st, pfx + "acc", [128, 4, TG], F32)
        sq_t, sq = c.sb(st, pfx + "sq", [128, 4, TG], F32)
        mean_t, mean = c.sb(st, pfx + "mean", [128, TG], F32)
        var_t, var = c.sb(st, pfx + "var", [128, TG], F32)
        rstd_t, rstd = c.sb(st, pfx + "rstd", [128, TG], F32)
        tn_r = c.rot(st, pfx + "tn", [128, TG], F32, 2)
        ya_t, ya = c.sb(st, pfx + "ya", [128, 4, TG], BF16)
        yb_t, yb = c.sb(st, pfx + "yb", [128, 4, TG], BF16)
        yc_t, ycs = c.sb(st, pfx + "yc", [128, 4, TG], BF16)
        uT_t, uT = c.sb(st, pfx + "uT", [128, 4, TG], BF16)
        vg_r = c.rot(st, pfx + "vg", [128, 512], F32, 2)
        vl_r = c.rot(st, pfx + "vl", [128, 512], BF16, 2)
        bst_r = c.rot(st, pfx + "bst", [128, 6], F32, 3)
        bmv_r = c.rot(st, pfx + "bmv", [128, 2], F32, 3)
        gt_r = c.rot(st, pfx + "gt", [128, 512], F32, 4)
        wuv_t, wuv = c.sb(st, pfx + "wuv", [128, 8, 1024], BF16)
        wo_t, wo = c.sb(st, pfx + "wout", [128, 8, 1024], BF16)
        wgt_r = c.rot(st, pfx + "wgt", [128, 8, 512], BF16, 2)
        wb_r = c.rot(st, pfx + "wb", [128, 4, 512], BF16, 2)
        m_t, m = c.sb(st, pfx + "m", [128, 8, TG], F32)
        mg_t, mg = c.sb(st, pfx + "mg", [128, 8, TG], BF16)
        gs_r = c.rot(st, pfx + "gs", [128, TG], F32, 2)
        tp_r = c.rot(st, pfx + "tp", [128, TG], F32, 2)
        lg_t, lg = c.sb(st, pfx + "lng", [128, 512], F32)
        lb_t, lb = c.sb(st, pfx + "lnb", [128, 512], F32)
        bs_t, bs = c.sb(st, pfx + "bs", [128, 512], F32)
        wsf_t, wsf = c.sb(st, pfx + "wsf", [128, 4, 128], F32)
        trl_t, trl = c.sb(st, pfx + "tril", [128, 128], F32)
        ws_t, ws = c.sb(st, pfx + "ws", [128, 4, 128], BF16)
        on_t, ones = c.sb(st, pfx + "ones", [128, 128], F32)

        c.dma(gam[:], norm_d.to_broadcast([128, 1024]), [], [gam_t], pfx + "gam")
        c.dma(pp[:], ppd, [], [pp_t], pfx + "pp")
        c.dma(lg[:], sgu_g_d.to_broadcast([128, 512]), [], [lg_t], pfx + "lg")
        c.dma(lb[:], sgu_bb_d.to_broadcast([128, 512]), [], [lb_t], pfx + "lb")
        c.dma(bs[:], sgu_b_d.to_broadcast([128, 512]), [], [bs_t], pfx + "bs")
        c.dma(wsf[:], sgu_wT_d.rearrange("g s t -> s g t"), [], [wsf_t], pfx + "wsf")
        c.dma(trl[:], tril_d, [], [trl_t], pfx + "trl")
        for gi in range(4):
            c.tt(ws[:, gi, :], wsf[:, gi, :], trl[:], ALU.mult, [wsf_t, trl_t], [ws_t])
        c.memset(ones[:], 1.0 / 512.0, [on_t])
        for j in range(2):
            c.dma(wuv[:, :, j * 512:(j + 1) * 512],
                  w_uv[:, j * 512:(j + 1) * 512].rearrange("(kc p) n -> p kc n", p=128), [], [wuv_t],
                  pfx + "wuv", q="pool")
            c.dma(wo[:, :, j * 512:(j + 1) * 512],
                  w_out[:, j * 512:(j + 1) * 512].rearrange("(kc p) n -> p kc n", p=128), [], [wo_t],
                  pfx + "wout", q="pool")
        cw = lambda ch, k: pp[:, ch * 31 + k:ch * 31 + k + 1]

        def conv_ops(g):
            ops = [lambda: c.dma(ygh[:], ygh_d.rearrange("(c p) t -> p c t", p=128)[:, :, g * TG:(g + 1) * TG + 30],
                                 [], [ygh_t], pfx + "ygh", q="pool")]
            pc_t, pc = c.psum[7]
            for ch in range(4):
                ops.append(lambda ch=ch: c.dma(cdg[:, 0:16, :], cdiag_d[ch, :, 0:16 * 128].rearrange(
                    "p (k n) -> p k n", n=128), [], [cdg_t], pfx + "cdg", q="pool"))
                ops.append(lambda ch=ch: c.dma(cdg[:, 16:31, :], cdiag_d[ch, :, 16 * 128:31 * 128].rearrange(
                    "p (k n) -> p k n", n=128), [], [cdg_t], pfx + "cdg", q="pool"))
                for k in range(31):
                    ops.append(lambda ch=ch, k=k: c.mm(pc_t, pc[:, :], cdg[:, k, :], ygh[:, ch, k:k + TG], k == 0, k == 30,
                                                       [cdg_t, ygh_t]))
                ops.append(lambda ch=ch: c.act(acc[:, ch, :], pc[:, :], AF.Identity, [pc_t, pp_t], [acc_t],
                                               bias=pp[:, 124 + ch:125 + ch]))
            return ops

        def gload(i, hh):
            g_t, gw = wgt_r.next()
            b_t, bw = wb_r.next()
            k = (wgt_r.i - 1) % 2
            c0 = i * 1024 + hh * 512
            c.dma(gw[:], w_gate[:, c0:c0 + 512].rearrange("(kc p) n -> p kc n", p=128), [], [g_t],
                  f"{pfx}wgt{k}", q="pool")
            c.dma(bw[:], w_branch[i, :, hh * 512:(hh + 1) * 512].rearrange("(kc p) n -> p kc n", p=128), [], [b_t],
                  f"{pfx}wb{k}", q="pool")
            return (g_t, gw, b_t, bw)

        nxtw = gload(0, 0)
        for g in range(NG):
            xg_t, xg = xg_r.next()
            xk = f"{pfx}xg0"
            c.dma(xg[:], xview(x_src, g), [xt_src[g]], [xg_t], xk)
            c.dma(ycs[:], ycT_d.rearrange("(c p) t -> p c t", p=128)[:, :, g * TG:(g + 1) * TG], [], [yc_t],
                  pfx + "ycs")
            rmsnorm_group(c, R, xg_t, xg, gam_t, gam, hT_t, hT)
            if g == 0:
                for f in conv_ops(0):
                    f()
            c.act(sq[:], acc[:], AF.Square, [acc_t], [sq_t])
            pm_t, pm = c.ps()
            pe_t, pe2 = c.ps()
            for ch in range(4):
                c.mm(pm_t, pm[:, :], ones[:], acc[:, ch, :], ch == 0, ch == 3, [on_t, acc_t])
            for ch in range(4):
                c.mm(pe_t, pe2[:, :], ones[:], sq[:, ch, :], ch == 0, ch == 3, [on_t, sq_t])
            c.cp(mean[:], pm[:, :], [pm_t], [mean_t], eng="act")
            c.tt(var[:], mean[:], mean[:], ALU.mult, [mean_t], [var_t])
            c.tt(var[:], pe2[:, :], var[:], ALU.subtract, [pe_t, var_t], [var_t])
            c.ts(var[:], var[:], EPS, None, ALU.add, None, [var_t], [var_t])
            c.act(rstd[:], var[:], AF.Sqrt, [var_t], [rstd_t])
            c.op("dve", lambda e: e.reciprocal(out=rstd[:], in_=rstd[:]), [rstd_t], [rstd_t])
            for ch in range(4):
                tn_t, tn = tn_r.next()
                c.tt(tn[:], acc[:, ch, :], mean[:], ALU.subtract, [acc_t, mean_t], [tn_t])
                c.tt(tn[:], tn[:], rstd[:], ALU.mult, [tn_t, rstd_t], [tn_t])
                c.act(ya[:, ch, :], tn[:], AF.Silu, [tn_t, pp_t], [ya_t],
                      bias=pp[:, 132 + ch:133 + ch], scale=pp[:, 128 + ch:129 + ch])
            K1_ = 1.5957691216057308

            def gelu_p1(src_ap, src_t):
                a_t, a_ = gt_r.next()
                b_t, b_ = gt_r.next()
                c.act(a_[:], src_ap, AF.Square, [src_t], [a_t])
                c.ts(a_[:], a_[:], 0.044715, 1.0, ALU.mult, ALU.add, [a_t], [a_t])
                c.tt(b_[:], a_[:], src_ap, ALU.mult, [a_t, src_t], [b_t])
                c.act(a_[:], b_[:], AF.Sigmoid, [b_t], [a_t], scale=K1_)
                return a_t, a_

            uS = {}

            def uA(ch):
                pu_t, pu = c.ps()
                for kc in range(8):
                    c.mm(pu_t, pu[:, :], wuv[:, kc, ch * 128:(ch + 1) * 128], hT[:, kc, :], kc == 0, kc == 7,
                         [wuv_t, hT_t])
                uS[ch] = (pu_t, pu) + gelu_p1(pu[:, :], pu_t)

            def uB(ch):
                pu_t, pu, a_t, a_ = uS.pop(ch)
                c.tt(uT[:, ch, :], a_[:], pu[:, :], ALU.mult, [a_t, pu_t], [uT_t])

            for step in range(4 + 1):
                if step < 4:
                    uA(step)
                if step >= 1:
                    uB(step - 1)

            vA_, vB_, vC_ = {}, {}, {}

            def vA(n):
                pv_t, pv = c.ps()
                for kc in range(8):
                    c.mm(pv_t, pv[:, :], hT[:, kc, n * 128:(n + 1) * 128], wuv[:, kc, 512:1024], kc == 0, kc == 7,
                         [hT_t, wuv_t])
                vA_[n] = (pv_t, pv) + gelu_p1(pv[:, :], pv_t)

            def vB(n):
                pv_t, pv, a_t, a_ = vA_.pop(n)
                vg_t, vg = vg_r.next()
                c.tt(vg[:], a_[:], pv[:, :], ALU.mult, [a_t, pv_t], [vg_t])
                bst_t, bst = bst_r.next()
                c.op("dve", lambda e, o=bst, i=vg: e.bn_stats(out=o[:], in_=i[:]), [vg_t], [bst_t])
                bmv_t, bmv = bmv_r.next()
                c.op("dve", lambda e, o=bmv, i=bst: e.bn_aggr(out=o[:], in_=i[:]), [bst_t], [bmv_t])
                c.ts(bmv[:, 1:2], bmv[:, 1:2], EPS, None, ALU.add, None, [bmv_t], [bmv_t])
                c.act(bmv[:, 1:2], bmv[:, 1:2], AF.Sqrt, [bmv_t], [bmv_t])
                vB_[n] = (vg_t, vg, bmv_t, bmv)

            def vC(n):
                vg_t, vg, bmv_t, bmv = vB_.pop(n)
                c.op("dve", lambda e, r=bmv: e.reciprocal(out=r[:, 1:2], in_=r[:, 1:2]), [bmv_t], [bmv_t])
                c.ts(vg[:], vg[:], bmv[:, 0:1], bmv[:, 1:2], ALU.subtract, ALU.mult, [vg_t, bmv_t], [vg_t])
                c.tt(vg[:], vg[:], lg[:], ALU.mult, [vg_t, lg_t], [vg_t])
                vl_t, vl = vl_r.next()
                c.tt(vl[:], vg[:], lb[:], ALU.add, [vg_t, lb_t], [vl_t])
                px_t, px = c.ps()
                for gi in range(4):
                    c.mm(px_t, px[:, gi * 128:(gi + 1) * 128], vl[:, gi * 128:(gi + 1) * 128], ws[:, gi, :],
                         True, True, [vl_t, ws_t])
                vC_[n] = (px_t, px)

            def vD(n):
                px_t, px = vC_.pop(n)
                tp_t, tp = tp_r.next()
                c.tt(tp[:], px[:, :], bs[:], ALU.add, [px_t, bs_t], [tp_t])
                c.tt(yb[:, :, n * 128:(n + 1) * 128], tp[:].rearrange("p (g t) -> p g t", t=128),
                     uT[:, :, n * 128:(n + 1) * 128], ALU.mult, [tp_t, uT_t], [yb_t])

            for step in range(4 + 3):
                if step < 4:
                    vA(step)
                if 0 <= step - 1 < 4:
                    vB(step - 1)
                if 0 <= step - 2 < 4:
                    vC(step - 2)
                if 0 <= step - 3 < 4:
                    vD(step - 3)
            ys = [(ya_t, ya), (yb_t, yb), (yc_t, ycs)]
            pend = conv_ops(g + 1) if g + 1 < NG else []
            for i in range(3):
                y_t, y = ys[i]
                for hh in range(2):
                    wgt_t, wgt, wb_t, wb = nxtw
                    step = (g * 3 + i) * 2 + hh + 1
                    if step < NG * 6:
                        nxtw = gload((step // 2) % 3, step % 2)
                    for dq in range(4):
                        dc = hh * 4 + dq
                        pz_t, pz = c.ps()
                        ppj_t, ppj = c.ps()
                        for kc in range(8):
                            c.mm(pz_t, pz[:, :], wgt[:, kc, dq * 128:(dq + 1) * 128], hT[:, kc, :], kc == 0, kc == 7,
                                 [wgt_t, hT_t])
                        for kc in range(4):
                            c.mm(ppj_t, ppj[:, :], wb[:, kc, dq * 128:(dq + 1) * 128], y[:, kc, :], kc == 0, kc == 3,
                                 [wb_t, y_t])
                        gs_t, gs = gs_r.next()
                        c.act(gs[:], pz[:, :], AF.Sigmoid, [pz_t, pp_t], [gs_t],
                              bias=pp[:, 136 + i * 8 + dc:137 + i * 8 + dc])
                        if i == 0:
                            c.tt(m[:, dc, :], gs[:], ppj[:, :], ALU.mult, [gs_t, ppj_t], [m_t])
                        else:
                            tp_t, tp = tp_r.next()
                            c.tt(tp[:], gs[:], ppj[:, :], ALU.mult, [gs_t, ppj_t], [tp_t])
                            if i == 1:
                                c.tt(m[:, dc, :], m[:, dc, :], tp[:], ALU.add, [m_t, tp_t], [m_t])
                            else:
                                c.tt(mg[:, dc, :], m[:, dc, :], tp[:], ALU.add, [m_t, tp_t], [mg_t])
                        for _ in range(6):
                            if pend:
                                pend.pop(0)()
            while pend:
                pend.pop(0)()
            for n in range(4):
                for half in range(2):
                    po_t, po = c.ps()
                    for kc in range(8):
                        c.mm(po_t, po[:, :], mg[:, kc, n * 128:(n + 1) * 128],
                             wo[:, kc, half * 512:(half + 1) * 512], kc == 0, kc == 7, [mg_t, wo_t])
                    xs = xg[:, n, half * 512:(half + 1) * 512]
                    c.tt(xs, xs, po[:, :], ALU.add, [xg_t, po_t], [xg_t])
            c.dma(xview(x_dst, g), xg[:], [xg_t], [xt_dst[g]], xk)
        c.end([pfx + "xg0"])
        c.ps_n = 8


def stage_final(c, pfx, x_src, xt_src, norm_d, out_d):
    with contextlib.ExitStack() as st:
        c.begin()
        xg_r = c.rot(st, pfx + "xg", [128, 4, 1024], F32, 2)
        R = norm_rots(c, st, pfx)
        gam_t, gam = c.sb(st, pfx + "gam", [128, 1024], F32)
        c.dma(gam[:], norm_d.to_broadcast([128, 1024]), [], [gam_t], pfx + "gam")
        for g in range(NG):
            xg_t, xg = xg_r.next()
            xk = f"{pfx}xg{(xg_r.i - 1) % 2}"
            c.dma(xg[:], xview(x_src, g), [xt_src[g]], [xg_t], xk)
            pend = {}

            def A(n, xg=xg, xg_t=xg_t):
                st_t, stt_ = R["st"].next()
                c.op("dve", lambda e, o=stt_, x=xg, n=n: e.bn_stats(out=o[:, 0:6], in_=x[:, n, 0:512]), [xg_t], [st_t])
                c.op("dve", lambda e, o=stt_, x=xg, n=n: e.bn_stats(out=o[:, 6:12], in_=x[:, n, 512:1024]), [xg_t],
                     [st_t])
                mv_t, mv = R["mv"].next()
                c.op("dve", lambda e, o=mv, i=stt_: e.bn_aggr(out=o[:, 0:2], in_=i[:, 0:12]), [st_t], [mv_t])
                ms_t, msq = R["ms"].next()
                c.stt(msq[:], mv[:, 0:1], mv[:, 0:1], mv[:, 1:2], ALU.mult, ALU.add, [mv_t], [ms_t])
                c.ts(msq[:], msq[:], EPS, None, ALU.add, None, [ms_t], [ms_t])
                rs_t, rs = R["rs"].next()
                c.act(rs[:], msq[:], AF.Sqrt, [ms_t], [rs_t])
                pend[n] = (rs_t, rs)

            def B(n, xg=xg, xg_t=xg_t):
                rs_t, rs = pend.pop(n)
                c.op("dve", lambda e, r=rs: e.reciprocal(out=r[:], in_=r[:]), [rs_t], [rs_t])
                c.stt(xg[:, n, :], xg[:, n, :], rs[:, 0:1], gam[:], ALU.mult, ALU.mult, [xg_t, rs_t, gam_t], [xg_t])

            for step in range(4 + 2):
                if step < 4:
                    A(step)
                if step >= 2:
                    B(step - 2)
            c.dma(xview(out_d, g), xg[:], [xg_t], [], xk)
        c.end([f"{pfx}xg0", f"{pfx}xg1"])


def build_tok(do_mix, do_p1, do_final, dbg=None):
    nc = bass.Bass("TRN2", target_bir_lowering=False)
    x_in = din(nc, "x_in", [NTOK, D], F32)
    identf_d = din(nc, "identf", [128, 128], F32)
    if do_mix:
        a = dict(
            mix_norm=din(nc, "a_mix_norm", [1, D], F32), w_uv=din(nc, "a_w_uv", [D, 1024], F32),
            w_gate=din(nc, "a_w_gate", [D, 3072], F32), w_branch=din(nc, "a_w_branch", [3, 512, D], F32),
            w_out=din(nc, "a_w_out", [D, D], F32), pp=din(nc, "a_pp", [128, NPP], F32),
            ygh=din(nc, "a_ygh", [512, NTOK + 30], F32), ycT=din(nc, "a_ycT", [512, NTOK], BF16),
            sgu_wT=din(nc, "a_sgu_wT", [4, 128, 128], F32), sgu_b=din(nc, "a_sgu_b", [1, 512], F32),
            sgu_g=din(nc, "a_sgu_g", [1, 512], F32), sgu_bb=din(nc, "a_sgu_bb", [1, 512], F32),
            tril=din(nc, "tril", [128, 128], F32), cdiag=din(nc, "a_cdiag", [4, 128, 31 * 128], F32),
            ffn2_norm=din(nc, "a_ffn2_norm", [1, D], F32), ffn2_wi=din(nc, "a_ffn2_wi", [D, 2 * DFF], F32),
            ffn2_wo=din(nc, "a_ffn2_wo", [DFF, D], F32))
    if do_p1:
        b = dict(
            ffn1_norm=din(nc, "b_ffn1_norm", [1, D], F32), ffn1_wi=din(nc, "b_ffn1_wi", [D, 2 * DFF], F32),
            ffn1_wo=din(nc, "b_ffn1_wo", [DFF, D], F32), mix_norm=din(nc, "b_mix_norm", [1, D], F32),
            w_qkv=din(nc, "b_w_qkv", [D, 1536], F32), w_glu=din(nc, "b_w_glu", [D, 1024], F32),
            cosr=din(nc, "cosr", [128, 1024], F32), sinr=din(nc, "sinr", [128, 1024], F32))
        q_o = dout(nc, "q_o", [NTOK, 512], BF16)
        k_o = dout(nc, "k_o", [NTOK, 512], BF16)
        v_o = dout(nc, "v_o", [NTOK, 512], BF16)
        yglu_o = dout(nc, "yglu_o", [512, NTOK], F32)
    if do_final:
        fin = din(nc, "final_norm", [1, D], F32)
    x_out = dout(nc, "x_out", [NTOK, D], F32)
    if do_mix:
        xs1 = nc.dram_tensor("xs1", [NTOK, D], F32, kind="Internal").ap()
        xs2 = nc.dram_tensor("xs2", [NTOK, D], F32, kind="Internal").ap()
    fresh = lambda nm: [T(f"{nm}{g}") for g in range(NG)]
    with contextlib.ExitStack() as st:
        c = Ctx(nc, st)
        cur = x_in
        if do_mix:
            stage_mixer(c, "m_", cur, xs1, fresh("a"), fresh("b"), a["mix_norm"], a["w_uv"], a["w_gate"],
                        a["w_branch"], a["w_out"], a["pp"], a["ygh"], a["ycT"], a["sgu_wT"], a["sgu_b"],
                        a["sgu_g"], a["sgu_bb"], a["tril"], identf_d, a["cdiag"])
            stage_ffn(c, "f2_", xs1, xs2, fresh("a"), fresh("b"), a["ffn2_norm"], a["ffn2_wi"], a["ffn2_wo"],
                      identf_d)
            cur = xs2
        if do_p1:
            if dbg != "noffn":
                stage_ffn(c, "f1_", cur, x_out, fresh("a"), fresh("b"), b["ffn1_norm"], b["ffn1_wi"], b["ffn1_wo"],
                          identf_d)
            if dbg != "noproj":
                    stage_proj(c, "p_", (x_in if dbg == "noffn" else x_out), fresh("a"), b["mix_norm"], b["w_qkv"], b["w_glu"], b["cosr"], b["sinr"],
                           q_o, k_o, v_o, yglu_o, identf_d)
        if do_final:
            stage_final(c, "fn_", cur, fresh("a"), fin, x_out)
    return nc


_NC = {}


def _get(name, fn):
    if name not in _NC:
        _NC[name] = fn()
    return _NC[name]


def _bf(a):
    return np.ascontiguousarray(a).astype(ml_dtypes.bfloat16) if a.dtype != ml_dtypes.bfloat16 else np.ascontiguousarray(a)


def _consts():
    f = np.float32
    identf = np.eye(128, dtype=f)
    tril = (np.arange(128)[:, None] <= np.arange(128)[None, :]).astype(f)
    pos = np.arange(SEQ, dtype=f)
    inv = (np.float32(500000.0) ** (-(np.arange(0, 16, 2, dtype=f)) / np.float32(16))).astype(f)
    ang = (pos[:, None] * inv[None, :]).astype(f)
    cosr = np.tile(np.cos(ang).astype(f), (1, 8))
    sinr = np.tile(np.sin(ang).astype(f), (1, 8))
    kaug = np.zeros((32, SEQ), dtype=f)
    for j in range(32):
        kaug[j, j * 256:(j + 1) * 256] = 1.0
    qb = np.arange(64) // 2
    pastneg = np.where(np.arange(32)[None, :] < qb[:, None], 0.0, -1e30).astype(f).reshape(1, 2048)
    ownm = (np.arange(32)[None, :] == qb[:, None]).astype(f).reshape(1, 2048)
    cm = np.zeros((4, 128, 512), dtype=f)
    for i in range(4):
        kk = i * 128 + np.arange(128)[:, None]
        cm[i] = np.where(kk <= np.arange(512)[None, :], 0.0, NEG)
    return dict(identf=identf, tril=tril, cosr=cosr, sinr=sinr, kaug=_bf(kaug), pastneg=pastneg, ownm=ownm,
                cmask=_bf(cm))


def _conv_diag(conv_w):
    d = np.zeros((4, 128, 31, 128), dtype=np.float32)
    p = np.arange(128)
    for ch in range(4):
        d[ch, p, :, p] = conv_w[:, ch * 128:(ch + 1) * 128].T
    return d.reshape(4, 128, 31 * 128)


def _pack_pp(conv_w, conv_b, ln_g, ln_b, gate_b):
    pp = np.zeros((128, NPP), dtype=np.float32)
    pp[:, 0:124] = conv_w.reshape(31, 4, 128).transpose(2, 1, 0).reshape(128, 124)
    pp[:, 124:128] = conv_b.reshape(4, 128).T
    pp[:, 128:132] = ln_g.reshape(4, 128).T
    pp[:, 132:136] = ln_b.reshape(4, 128).T
    pp[:, 136:160] = gate_b.reshape(3, 8, 128).transpose(2, 0, 1).reshape(128, 24)
    return pp


def _run(nc, maps):
    res = run_bass_kernel_spmd(nc, maps, core_ids=list(range(8)))
    return res.results


def kernel(x, ffn1_norm, ffn1_wi, ffn1_wo, mix_norm, w_in, conv_w, conv_b, conv_ln_g, conv_ln_b, sgu_ln_g,
           sgu_ln_b, sgu_w, sgu_b, w_branch, gate_b, w_out, ffn2_norm, ffn2_wi, ffn2_wo, final_norm):
    A = lambda v: np.ascontiguousarray(np.asarray(v, dtype=np.float32))
    x = A(x)
    P = dict(ffn1_norm=A(ffn1_norm), ffn1_wi=A(ffn1_wi), ffn1_wo=A(ffn1_wo), mix_norm=A(mix_norm), w_in=A(w_in),
             conv_w=A(conv_w), conv_b=A(conv_b), conv_ln_g=A(conv_ln_g), conv_ln_b=A(conv_ln_b),
             sgu_ln_g=A(sgu_ln_g), sgu_ln_b=A(sgu_ln_b), sgu_w=A(sgu_w), sgu_b=A(sgu_b), w_branch=A(w_branch),
             gate_b=A(gate_b), w_out=A(w_out), ffn2_norm=A(ffn2_norm), ffn2_wi=A(ffn2_wi), ffn2_wo=A(ffn2_wo))
    final_norm = A(final_norm)
    C = _consts()
    xs = [x[c // 4, (c % 4) * NTOK:(c % 4 + 1) * NTOK, :] for c in range(8)]

    def p1_inputs(l):
        w = P["w_in"][l]
        return dict(b_ffn1_norm=P["ffn1_norm"][l][None], b_ffn1_wi=P["ffn1_wi"][l], b_ffn1_wo=P["ffn1_wo"][l],
                    b_mix_norm=P["mix_norm"][l][None], b_w_qkv=A(w[:, 2048:3584]), b_w_glu=A(w[:, 0:1024]))

    def mix_inputs(l):
        w = P["w_in"][l]
        return dict(a_mix_norm=P["mix_norm"][l][None], a_w_uv=A(w[:, 1024:2048]), a_w_gate=A(w[:, 3584:6656]),
                    a_w_branch=P["w_branch"][l], a_w_out=P["w_out"][l],
                    a_pp=_pack_pp(P["conv_w"][l], P["conv_b"][l], P["conv_ln_g"][l], P["conv_ln_b"][l],
                                  P["gate_b"][l]),
                    a_cdiag=_conv_diag(P["conv_w"][l]),
                    a_sgu_wT=A(P["sgu_w"][l].transpose(0, 2, 1)), a_sgu_b=P["sgu_b"][l].reshape(1, 512),
                    a_sgu_g=P["sgu_ln_g"][l][None], a_sgu_bb=P["sgu_ln_b"][l][None], tril=C["tril"],
                    a_ffn2_norm=P["ffn2_norm"][l][None], a_ffn2_wi=P["ffn2_wi"][l], a_ffn2_wo=P["ffn2_wo"][l])

    def rope_tabs(c):
        o = (c % 4) * NTOK
        pm = lambda t: A(t[o:o + NTOK].reshape(16, 128, 64).transpose(1, 0, 2).reshape(128, 1024))
        return dict(cosr=pm(C["cosr"]), sinr=pm(C["sinr"]))

    def attention(res):
        q = np.stack([np.concatenate([res[b * 4 + j]["q_o"] for j in range(4)], 0) for b in range(2)])
        k = np.stack([np.concatenate([res[b * 4 + j]["k_o"] for j in range(4)], 0) for b in range(2)])
        v = np.stack([np.concatenate([res[b * 4 + j]["v_o"] for j in range(4)], 0) for b in range(2)])
        maps = []
        for h in range(8):
            sl = slice(h * 64, (h + 1) * 64)
            maps.append(dict(qT=_bf(q[:, :, sl].transpose(0, 2, 1)), kT=_bf(k[:, :, sl].transpose(0, 2, 1)),
                             v=_bf(v[:, :, sl]), kaug=C["kaug"], pastneg=C["pastneg"], ownm=C["ownm"],
                             cmask=C["cmask"], identf=C["identf"]))
        r = _run(_get("att", build_att), maps)
        Y = np.concatenate([np.asarray(r[h]["yc"]) for h in range(8)], axis=2)
        G = [np.concatenate([np.asarray(res[b * 4 + j]["yglu_o"]) for j in range(4)], 1) for b in range(2)]
        ycT, ygh = [], []
        for c in range(8):
            b, o = c // 4, (c % 4) * NTOK
            ycT.append(_bf(Y[b, o:o + NTOK, :].T))
            gh = np.zeros((512, NTOK + 30), dtype=np.float32)
            gh[:, 30:] = G[b][:, o:o + NTOK]
            if o > 0:
                gh[:, :30] = G[b][:, o - 30:o]
            ygh.append(gh)
        return ycT, ygh

    maps = [dict(x_in=xs[c], identf=C["identf"], **p1_inputs(0), **rope_tabs(c)) for c in range(8)]
    res = _run(_get("k1", lambda: build_tok(False, True, False)), maps)
    xcur = [np.asarray(res[c]["x_out"]) for c in range(8)]
    ycT, ygh = attention(res)
    maps = [dict(x_in=xcur[c], identf=C["identf"], a_ygh=ygh[c], a_ycT=ycT[c], **mix_inputs(0), **p1_inputs(1),
                 **rope_tabs(c)) for c in range(8)]
    res = _run(_get("k2", lambda: build_tok(True, True, False)), maps)
    xcur = [np.asarray(res[c]["x_out"]) for c in range(8)]
    ycT, ygh = attention(res)
    maps = [dict(x_in=xcur[c], identf=C["identf"], a_ygh=ygh[c], a_ycT=ycT[c], final_norm=final_norm[None],
                 **mix_inputs(1)) for c in range(8)]
    res = _run(_get("k3", lambda: build_tok(True, False, True)), maps)
    out = np.stack([np.concatenate([np.asarray(res[b * 4 + j]["x_out"]) for j in range(4)], 0) for b in range(2)])
    return out.astype(np.float32)
```
